# Optimizing a Trainium2 kernel written in Bass

```python
import math
import jax, jax.numpy as jnp
from jax import lax
import numpy as np

D_MODEL = 1024
BATCH = 8
SEQ = 4096
DEPTH = 2

N_MIXERS = 2
HEAD_DIM = 128
GDN_HEADS = D_MODEL // HEAD_DIM
GDN_CONV = 4
GDN_CHUNK = 64
NSA_Q_HEADS = D_MODEL // HEAD_DIM
NSA_KV_HEADS = NSA_Q_HEADS // 4
CMP_LEN = 32
CMP_STRIDE = 16
SLC_BLOCK = 64
SLC_TOPK = 16
WINDOW = 512
NSA_Q_BLOCK = 64
MEM_TOKENS = 256
MEM_HEADS = 4
NUM_BUCKETS = 32
MAX_DISTANCE = 128
D_FF = ((8 * D_MODEL // 3 + 255) // 256) * 256
FFN_CONV = 3
MEM_WIDTH = MEM_HEADS * HEAD_DIM
MIX_WIDTH = D_MODEL + MEM_WIDTH
GDN_COLS = 4 * GDN_HEADS * HEAD_DIM + 2 * GDN_HEADS
NSA_KV_WIDTH = NSA_KV_HEADS * HEAD_DIM
NSA_COLS = NSA_Q_HEADS * HEAD_DIM + 6 * NSA_KV_WIDTH + 3 * NSA_Q_HEADS
N_GDN = (DEPTH + 1) // 2
N_NSA = DEPTH // 2
EPS = 1e-6

kernel_name = 'hybrid_gdn_nsa_memory_convglu'


def rms_norm(x, w):
    xf = x.astype(jnp.float32)
    y = xf * lax.rsqrt(jnp.mean(xf * xf, axis=-1, keepdims=True) + EPS)
    return (y * w.astype(jnp.float32)).astype(x.dtype)


def l2norm(x):
    xf = x.astype(jnp.float32)
    return xf * lax.rsqrt(jnp.sum(xf * xf, axis=-1, keepdims=True) + EPS)


def causal_dwconv(x, w):
    K = w.shape[0]
    S = x.shape[1]
    xp = jnp.pad(x, ((0, 0), (K - 1, 0), (0, 0)))
    y = xp[:, 0:S] * w[0]
    for j in range(1, K):
        y = y + xp[:, j:j + S] * w[j]
    return y


def t5_bucket(dist):
    n = jnp.maximum(dist, 0)
    max_exact = NUM_BUCKETS // 2
    logv = jnp.log(jnp.maximum(n, 1).astype(jnp.float32) / max_exact) / math.log(MAX_DISTANCE / max_exact)
    large = max_exact + (logv * (NUM_BUCKETS - max_exact)).astype(jnp.int32)
    large = jnp.minimum(large, NUM_BUCKETS - 1)
    return jnp.where(n < max_exact, n, large)


def masked_softmax(s, mask):
    s = jnp.where(mask, s.astype(jnp.float32), -jnp.inf)
    m = jnp.max(s, axis=-1, keepdims=True)
    m = jnp.where(jnp.isfinite(m), m, 0.0)
    p = jnp.where(mask, jnp.exp(s - m), 0.0)
    return p / jnp.maximum(jnp.sum(p, axis=-1, keepdims=True), 1e-30)


def gated_delta_rule_chunked(q, k, v, g, beta):
    f32 = jnp.float32
    B, H, S, dk = q.shape
    dv = v.shape[-1]
    C = GDN_CHUNK
    N = S // C
    q = (q.astype(f32) * dk ** -0.5).reshape(B, H, N, C, dk)
    k = k.astype(f32).reshape(B, H, N, C, dk)
    v = v.astype(f32).reshape(B, H, N, C, dv)
    beta = beta.astype(f32).reshape(B, H, N, C)
    gc = jnp.cumsum(g.astype(f32).reshape(B, H, N, C), axis=-1)
    incl = jnp.tril(jnp.ones((C, C), bool))
    strict = jnp.tril(jnp.ones((C, C), bool), -1)
    diff = gc[..., :, None] - gc[..., None, :]
    decay = jnp.where(incl, jnp.exp(jnp.where(incl, diff, 0.0)), 0.0)
    kb = k * beta[..., None]
    L = jnp.where(strict, jnp.einsum('bhnid,bhnjd->bhnij', kb, k) * decay, 0.0)
    eye = jnp.eye(C, dtype=f32)
    T = lax.linalg.triangular_solve(eye + L, jnp.broadcast_to(eye, L.shape),
                                    left_side=True, lower=True, unit_diagonal=True)
    u = jnp.einsum('bhnij,bhnjd->bhnid', T, v * beta[..., None])
    w = jnp.einsum('bhnij,bhnjd->bhnid', T, kb * jnp.exp(gc)[..., None])
    a_intra = jnp.where(incl, jnp.einsum('bhnid,bhnjd->bhnij', q, k) * decay, 0.0)
    q_dec = q * jnp.exp(gc)[..., None]
    k_dec = k * jnp.exp(gc[..., -1:] - gc)[..., None]
    g_last = jnp.exp(gc[..., -1])

    def chunk_step(state, xs):
        u_n, w_n, a_n, qd_n, kd_n, gl_n = xs
        v_new = u_n - jnp.einsum('bhcd,bhde->bhce', w_n, state)
        o_n = jnp.einsum('bhcd,bhde->bhce', qd_n, state) + jnp.einsum('bhij,bhje->bhie', a_n, v_new)
        state = state * gl_n[..., None, None] + jnp.einsum('bhcd,bhce->bhde', kd_n, v_new)
        return state, o_n

    to_n = lambda t: jnp.moveaxis(t, 2, 0)
    state0 = jnp.zeros((B, H, dk, dv), f32)
    _, o = lax.scan(chunk_step, state0,
                    (to_n(u), to_n(w), to_n(a_intra), to_n(q_dec), to_n(k_dec), to_n(g_last)))
    return jnp.moveaxis(o, 0, 2).reshape(B, H, S, dv)


def gdn_mixer(cols, conv_w, a_log, dt_bias, norm_w):
    B, S, _ = cols.shape
    H, dh = GDN_HEADS, HEAD_DIM
    wd = H * dh
    qkv = jax.nn.silu(causal_dwconv(cols[..., :3 * wd], conv_w)).reshape(B, S, 3, H, dh)
    q, k, v = qkv[:, :, 0], qkv[:, :, 1], qkv[:, :, 2]
    z = cols[..., 3 * wd:4 * wd].reshape(B, S, H, dh)
    b = cols[..., 4 * wd:4 * wd + H]
    a = cols[..., 4 * wd + H:4 * wd + 2 * H]
    beta = jax.nn.sigmoid(b.astype(jnp.float32))
    g = -jnp.exp(a_log.astype(jnp.float32)) * jax.nn.softplus(a.astype(jnp.float32) + dt_bias.astype(jnp.float32))
    heads = lambda t: jnp.swapaxes(t, 1, 2)
    o = gated_delta_rule_chunked(heads(l2norm(q)), heads(l2norm(k)), heads(v.astype(jnp.float32)),
                                 heads(g), heads(beta))
    o = jnp.swapaxes(o, 1, 2)
    o = rms_norm(o, norm_w) * jax.nn.silu(z.astype(jnp.float32))
    return o.reshape(B, S, wd).astype(cols.dtype)


def compress_blocks(t, pos_emb, w1, w2):
    B, S, G, dh = t.shape
    n_cmp = (S - CMP_LEN) // CMP_STRIDE + 1
    idx = (np.arange(n_cmp, dtype=np.int32)[:, None] * CMP_STRIDE
           + np.arange(CMP_LEN, dtype=np.int32)[None, :])
    blk = t[:, idx] + pos_emb[:, None, :]
    flat = jnp.transpose(blk, (0, 3, 1, 2, 4)).reshape(B, G, n_cmp, CMP_LEN * dh)
    return jax.nn.silu(flat @ w1) @ w2


def nsa_mixer(cols, rel_bias, pos_k, w1_k, w2_k, pos_v, w1_v, w2_v):
    B, S, _ = cols.shape
    HQ, G, dh = NSA_Q_HEADS, NSA_KV_HEADS, HEAD_DIM
    R = HQ // G
    Q = NSA_Q_BLOCK
    scale = dh ** -0.5
    qw = HQ * dh
    q = cols[..., :qw]
    kvs = [cols[..., qw + j * NSA_KV_WIDTH:qw + (j + 1) * NSA_KV_WIDTH].reshape(B, S, G, dh) for j in range(6)]
    k_cmp, v_cmp, k_slc, v_slc, k_win, v_win = kvs
    gates = jax.nn.sigmoid(cols[..., qw + 6 * NSA_KV_WIDTH:].astype(jnp.float32)).reshape(B, S, HQ, 3)

    n_q = S // Q
    n_cmp = (S - CMP_LEN) // CMP_STRIDE + 1
    n_slc = S // SLC_BLOCK
    n_sel = min(SLC_TOPK, n_slc)

    kc = compress_blocks(k_cmp, pos_k, w1_k, w2_k)
    vc = compress_blocks(v_cmp, pos_v, w1_v, w2_v)
    c_start = np.arange(n_cmp, dtype=np.int32) * CMP_STRIDE
    c_end = c_start + CMP_LEN - 1
    s_start = np.arange(n_slc, dtype=np.int32) * SLC_BLOCK
    sel_map = ((c_start[:, None] < s_start[None, :] + SLC_BLOCK)
               & (s_start[None, :] <= c_end[:, None])).astype(np.float32)
    to_blocks = lambda t: jnp.swapaxes(t, 1, 2).reshape(B, G, n_slc, SLC_BLOCK, dh)
    ks, vs = to_blocks(k_slc), to_blocks(v_slc)
    pad = lambda t: jnp.pad(jnp.swapaxes(t, 1, 2), ((0, 0), (0, 0), (WINDOW, 0), (0, 0)))
    kw, vw = pad(k_win), pad(v_win)
    tab = rel_bias.reshape(NUM_BUCKETS, G, R)
    c_end_j = jnp.asarray(c_end)
    bi = jnp.arange(B)[:, None, None, None]
    gi = jnp.arange(G)[None, :, None, None]

    def head_bias(dist):
        return jnp.moveaxis(tab[t5_bucket(dist)], (-2, -1), (0, 1))

    qh = jnp.moveaxis(q.reshape(B, S, G, R, dh).transpose(0, 2, 3, 1, 4).reshape(B, G, R, n_q, Q, dh), 3, 0)
    gh = jnp.moveaxis(gates.reshape(B, S, G, R, 3).transpose(0, 2, 3, 1, 4).reshape(B, G, R, n_q, Q, 3), 3, 0)

    def step(args):
        qi, qb, gb = args
        s0 = qi * Q
        t = s0 + jnp.arange(Q, dtype=jnp.int32)
        dist_c = t[:, None] - c_end_j[None, :]
        sc = jnp.einsum('bgrqd,bgcd->bgrqc', qb, kc).astype(jnp.float32) * scale + head_bias(dist_c)
        p_c = masked_softmax(sc, dist_c >= 0)
        o_c = jnp.einsum('bgrqc,bgcd->bgrqd', p_c, vc.astype(jnp.float32))
        imp = jnp.einsum('bgqc,cs->bgqs', jnp.sum(p_c, axis=2), sel_map)
        jb = jnp.arange(n_slc, dtype=jnp.int32)[None, :]
        tb = (t // SLC_BLOCK)[:, None]
        forced = (jb == 0) | (jb == tb) | (jb == tb - 1)
        score = jnp.where(jb <= tb, jnp.where(forced, jnp.inf, imp), -jnp.inf)
        vals, idx = lax.top_k(score, n_sel)
        ok = vals > -jnp.inf
        kg = ks[bi, gi, idx].reshape(B, G, Q, n_sel * SLC_BLOCK, dh)
        vg = vs[bi, gi, idx].reshape(B, G, Q, n_sel * SLC_BLOCK, dh)
        pos = idx[..., None] * SLC_BLOCK + jnp.arange(SLC_BLOCK, dtype=jnp.int32)
        dist_s = t[:, None, None] - pos
        mask_s = (ok[..., None] & (dist_s >= 0)).reshape(B, G, Q, n_sel * SLC_BLOCK)
        bias_s = jnp.moveaxis(tab[t5_bucket(dist_s).reshape(B, G, Q, n_sel * SLC_BLOCK), gi], -1, 2)
        ss = jnp.einsum('bgrqd,bgqkd->bgrqk', qb, kg).astype(jnp.float32) * scale + bias_s
        p_s = masked_softmax(ss, mask_s[:, :, None])
        o_s = jnp.einsum('bgrqk,bgqkd->bgrqd', p_s, vg.astype(jnp.float32))
        kwb = lax.dynamic_slice_in_dim(kw, s0, Q + WINDOW, axis=2)
        vwb = lax.dynamic_slice_in_dim(vw, s0, Q + WINDOW, axis=2)
        kpos = s0 - WINDOW + jnp.arange(Q + WINDOW, dtype=jnp.int32)
        dist_w = t[:, None] - kpos[None, :]
        mask_w = (dist_w >= 0) & (dist_w < WINDOW) & (kpos[None, :] >= 0)
        sw = jnp.einsum('bgrqd,bgkd->bgrqk', qb, kwb).astype(jnp.float32) * scale + head_bias(dist_w)
        p_w = masked_softmax(sw, mask_w)
        o_w = jnp.einsum('bgrqk,bgkd->bgrqd', p_w, vwb.astype(jnp.float32))
        o = gb[..., 0:1] * o_c + gb[..., 1:2] * o_s + gb[..., 2:3] * o_w
        return o.astype(cols.dtype)

    out = lax.map(step, (jnp.arange(n_q, dtype=jnp.int32), qh, gh))
    return jnp.transpose(out, (1, 0, 4, 2, 3, 5)).reshape(B, S, HQ * dh)


def memory_attention(qm, mem_n, w_kv):
    B, S = qm.shape[:2]
    kv = (mem_n @ w_kv).reshape(B, -1, 2, MEM_HEADS, HEAD_DIM)
    k, v = kv[:, :, 0], kv[:, :, 1]
    s = jnp.einsum('bshd,bmhd->bhsm', qm, k).astype(jnp.float32) * HEAD_DIM ** -0.5
    p = jax.nn.softmax(s, axis=-1)
    o = jnp.einsum('bhsm,bmhd->bshd', p, v.astype(jnp.float32))
    return o.reshape(B, S, MEM_WIDTH).astype(qm.dtype)


def conv_glu(h, w_up, conv_w, conv_b, w_down):
    gate, val = jnp.split(h @ w_up, 2, axis=-1)
    gate = causal_dwconv(gate, conv_w) + conv_b
    return (jax.nn.silu(gate) * val) @ w_down


def setup_inputs(seed: int = 0) -> dict:
    key = jax.random.key(seed)
    ks = iter(jax.random.split(key, 32))
    f32 = jnp.float32
    nrm = lambda shape, scale: scale * jax.random.normal(next(ks), shape, f32)
    gain = lambda shape: 1.0 + nrm(shape, 0.02)
    x = nrm((BATCH, SEQ, D_MODEL), 1.0)
    mem = nrm((BATCH, MEM_TOKENS, D_MODEL), 1.0)
    rel_bias = nrm((NUM_BUCKETS, NSA_Q_HEADS), 0.2)
    norm_mix_w = gain((DEPTH, D_MODEL))
    norm_ffn_w = gain((DEPTH, D_MODEL))
    final_norm_w = gain((D_MODEL,))
    mem_norm_w = gain((DEPTH, D_MODEL))
    mem_w_kv = nrm((DEPTH, D_MODEL, 2 * MEM_WIDTH), D_MODEL ** -0.5)
    w_out = nrm((DEPTH, MIX_WIDTH, D_MODEL), MIX_WIDTH ** -0.5)
    gdn_w_in = nrm((N_GDN, D_MODEL, GDN_COLS + MEM_WIDTH), D_MODEL ** -0.5)
    gdn_conv_w = nrm((N_GDN, GDN_CONV, 3 * GDN_HEADS * HEAD_DIM), GDN_CONV ** -0.5)
    gdn_a_log = jnp.log(jax.random.uniform(next(ks), (N_GDN, GDN_HEADS), f32, 1.0, 16.0))
    dt = jnp.exp(jax.random.uniform(next(ks), (N_GDN, GDN_HEADS), f32, math.log(1e-3), math.log(1e-1)))
    gdn_dt_bias = dt + jnp.log(-jnp.expm1(-dt))
    gdn_norm_w = gain((N_GDN, HEAD_DIM))
    nsa_w_in = nrm((N_NSA, D_MODEL, NSA_COLS + MEM_WIDTH), D_MODEL ** -0.5)
    nsa_cmp_pos_k = nrm((N_NSA, CMP_LEN, HEAD_DIM), 0.1)
    nsa_cmp_w1_k = nrm((N_NSA, CMP_LEN * HEAD_DIM, HEAD_DIM), (CMP_LEN * HEAD_DIM) ** -0.5)
    nsa_cmp_w2_k = nrm((N_NSA, HEAD_DIM, HEAD_DIM), HEAD_DIM ** -0.5)
    nsa_cmp_pos_v = nrm((N_NSA, CMP_LEN, HEAD_DIM), 0.1)
    nsa_cmp_w1_v = nrm((N_NSA, CMP_LEN * HEAD_DIM, HEAD_DIM), (CMP_LEN * HEAD_DIM) ** -0.5)
    nsa_cmp_w2_v = nrm((N_NSA, HEAD_DIM, HEAD_DIM), HEAD_DIM ** -0.5)
    ffn_w_up = nrm((DEPTH, D_MODEL, 2 * D_FF), D_MODEL ** -0.5)
    ffn_conv_w = nrm((DEPTH, FFN_CONV, D_FF), FFN_CONV ** -0.5)
    ffn_conv_b = nrm((DEPTH, D_FF), 0.02)
    ffn_w_down = nrm((DEPTH, D_FF, D_MODEL), D_FF ** -0.5)
    return {'x': x, 'mem': mem, 'rel_bias': rel_bias, 'norm_mix_w': norm_mix_w,
            'norm_ffn_w': norm_ffn_w, 'final_norm_w': final_norm_w, 'mem_norm_w': mem_norm_w,
            'mem_w_kv': mem_w_kv, 'w_out': w_out, 'gdn_w_in': gdn_w_in, 'gdn_conv_w': gdn_conv_w,
            'gdn_a_log': gdn_a_log, 'gdn_dt_bias': gdn_dt_bias, 'gdn_norm_w': gdn_norm_w,
            'nsa_w_in': nsa_w_in, 'nsa_cmp_pos_k': nsa_cmp_pos_k, 'nsa_cmp_w1_k': nsa_cmp_w1_k,
            'nsa_cmp_w2_k': nsa_cmp_w2_k, 'nsa_cmp_pos_v': nsa_cmp_pos_v, 'nsa_cmp_w1_v': nsa_cmp_w1_v,
            'nsa_cmp_w2_v': nsa_cmp_w2_v, 'ffn_w_up': ffn_w_up, 'ffn_conv_w': ffn_conv_w,
            'ffn_conv_b': ffn_conv_b, 'ffn_w_down': ffn_w_down}


def reference(x, mem, rel_bias, norm_mix_w, norm_ffn_w, final_norm_w, mem_norm_w, mem_w_kv, w_out,
              gdn_w_in, gdn_conv_w, gdn_a_log, gdn_dt_bias, gdn_norm_w, nsa_w_in, nsa_cmp_pos_k,
              nsa_cmp_w1_k, nsa_cmp_w2_k, nsa_cmp_pos_v, nsa_cmp_w1_v, nsa_cmp_w2_v, ffn_w_up,
              ffn_conv_w, ffn_conv_b, ffn_w_down):
    B, S, _ = x.shape
    for i in range(DEPTH):
        h = rms_norm(x, norm_mix_w[i])
        mem_n = rms_norm(mem, mem_norm_w[i])
        j = i // N_MIXERS
        if i % N_MIXERS == 0:
            proj = h @ gdn_w_in[j]
            mix = gdn_mixer(proj[..., :GDN_COLS], gdn_conv_w[j], gdn_a_log[j], gdn_dt_bias[j], gdn_norm_w[j])
            qm = proj[..., GDN_COLS:]
        else:
            proj = h @ nsa_w_in[j]
            mix = nsa_mixer(proj[..., :NSA_COLS], rel_bias, nsa_cmp_pos_k[j], nsa_cmp_w1_k[j],
                            nsa_cmp_w2_k[j], nsa_cmp_pos_v[j], nsa_cmp_w1_v[j], nsa_cmp_w2_v[j])
            qm = proj[..., NSA_COLS:]
        mo = memory_attention(qm.reshape(B, S, MEM_HEADS, HEAD_DIM), mem_n, mem_w_kv[i])
        x = x + jnp.concatenate([mix, mo], axis=-1) @ w_out[i]
        x = x + conv_glu(rms_norm(x, norm_ffn_w[i]), ffn_w_up[i], ffn_conv_w[i], ffn_conv_b[i], ffn_w_down[i])
    return rms_norm(x, final_norm_w)
```

```python
import contextlib
import numpy as np
import ml_dtypes
import concourse.bass as bass
import concourse.mybir as mybir
from concourse.bass_utils import run_bass_kernel_spmd

F32 = mybir.dt.float32
BF16 = mybir.dt.bfloat16
ALU = mybir.AluOpType
AF = mybir.ActivationFunctionType
AX = mybir.AxisListType
PE, ACT, DVE, POOL, SP = "pe", "act", "dve", "pool", "sp"
ENGS = (PE, ACT, DVE, POOL, SP)


class Res:
    __slots__ = ("name", "last_w", "readers")

    def __init__(self, name):
        self.name = name
        self.last_w = None
        self.readers = []


class Op:
    __slots__ = ("eng", "fn", "reads", "writes", "is_dma", "deps", "needs_inc", "done", "extra")


class V:
    __slots__ = ("ap", "res")

    def __init__(self, ap, res):
        self.ap = ap
        self.res = res

    def __getitem__(self, key):
        return V(self.ap[key], self.res)

    def m(self, f):
        return V(f(self.ap), self.res)

    def bc(self, shape):
        return V(self.ap.to_broadcast(list(shape)), self.res)


class TT:
    def __init__(self, h, name):
        self.h = h
        self.res = Res(name)

    def __getitem__(self, key):
        return V(self.h[key], self.res)


class Sched:
    W = 20000
    M = 6

    def __init__(self, nc):
        self.nc = nc
        self.ops = []
        self.engs = {PE: nc.tensor, ACT: nc.scalar, DVE: nc.vector, POOL: nc.gpsimd, SP: nc.sync}
        self.sem_pool = {}
        self.cnt = {k: 0 for k in ENGS}
        self.dcnt = {k: 0 for k in ENGS}
        self.waited = {k: {} for k in ENGS}
        self.dq = {k: [] for k in ENGS}
        self.last_compute = {k: None for k in ENGS}
        self.recent_dma = []
        self.n_ops = 0
        self.n_wait = 0

    def op(self, eng, fn, reads=(), writes=(), dma=False, extra=()):
        o = Op()
        o.eng = eng
        o.fn = fn
        o.reads = [r for r in reads if r is not None]
        o.writes = [w for w in writes if w is not None]
        o.is_dma = dma
        o.deps = []
        o.needs_inc = False
        o.done = None
        o.extra = list(extra)
        self.ops.append(o)
        if dma:
            self.recent_dma.append(o)
        elif fn is not None:
            self.last_compute[eng] = o
        return o

    def barrier(self):
        deps = [o for o in self.last_compute.values() if o is not None] + list(self.recent_dma)
        for e in ENGS:
            self.op(e, None, extra=deps)
        self.recent_dma = []

    def _sem(self, key):
        if key not in self.sem_pool:
            self.sem_pool[key] = self.nc.alloc_semaphore("s_%s_%s" % key)
        return self.sem_pool[key]

    def flush(self):
        ops = self.ops
        self.ops = []
        for o in ops:
            deps = set(o.extra)
            for r in o.reads:
                if r.last_w is not None:
                    deps.add(r.last_w)
            for w in o.writes:
                if w.last_w is not None:
                    deps.add(w.last_w)
                for rd in w.readers:
                    deps.add(rd)
            for r in o.reads:
                if r not in o.writes:
                    r.readers.append(o)
            for w in o.writes:
                w.last_w = o
                w.readers = []
            final = []
            for d in deps:
                if d is o:
                    continue
                if (not d.is_dma) and (not o.is_dma) and d.eng == o.eng and o.eng == PE and o.fn is not None:
                    continue
                final.append(d)
            o.deps = final
            for d in final:
                if not d.is_dma:
                    assert d.done is None or d.needs_inc, "dependency on already-emitted op without inc"
                    d.needs_inc = True
        for o in ops:
            if o.fn is None:
                continue
            if o.is_dma:
                n = self.dcnt[o.eng]
                self.dcnt[o.eng] += 1
                o.done = (("d" + o.eng, n % self.M), 16 * (n // self.M + 1))
            elif o.needs_inc:
                n = self.cnt[o.eng]
                self.cnt[o.eng] += 1
                o.done = ((o.eng, n // self.W), n % self.W + 1)
        for o in ops:
            e = self.engs[o.eng]
            wt = self.waited[o.eng]
            need = {}
            for d in o.deps:
                if d.done is None:
                    continue
                key, val = d.done
                if need.get(key, 0) < val:
                    need[key] = val
            if o.is_dma:
                q = self.dq[o.eng]
                if len(q) >= self.M:
                    key, val = q[len(q) - self.M].done
                    if need.get(key, 0) < val:
                        need[key] = val
                q.append(o)
            for key, val in need.items():
                if wt.get(key, 0) >= val:
                    continue
                wt[key] = val
                e.wait_ge(self._sem(key), val)
                self.n_wait += 1
            if o.fn is None:
                continue
            ins = o.fn(e)
            self.n_ops += 1
            if o.is_dma:
                ins.then_inc(self._sem(o.done[0]), 16)
            elif o.needs_inc:
                ins.then_inc(self._sem(o.done[0]), 1)


class KB:
    def __init__(self, nc):
        self.nc = nc
        self.S = Sched(nc)
        self.stack = None
        self.uid = 0

    @contextlib.contextmanager
    def phase(self, name):
        with contextlib.ExitStack() as st:
            prev = self.stack
            self.stack = st
            self.pname = name
            yield
            self.S.barrier()
            self.S.flush()
            self.stack = prev

    def sb(self, name, shape, dt=F32):
        self.uid += 1
        h = self.stack.enter_context(self.nc.sbuf_tensor("%s_%s_%d" % (self.pname, name, self.uid), list(shape), dt))
        return TT(h, name)

    def ps(self, name, shape=(128, 512), dt=F32):
        self.uid += 1
        h = self.stack.enter_context(self.nc.psum_tensor("%s_%s_%d" % (self.pname, name, self.uid), list(shape), dt))
        return TT(h, name)

    def dma(self, eng, out, in_, **kw):
        rd = [in_.res] if isinstance(in_, V) else []
        wr = [out.res] if isinstance(out, V) else []
        oa = out.ap if isinstance(out, V) else out
        ia = in_.ap if isinstance(in_, V) else in_
        self.S.op(eng, lambda e: e.dma_start(out=oa, in_=ia, **kw), rd, wr, dma=True)

    def mm(self, out, lhsT, rhs, start=True, stop=True, **kw):
        self.S.op(PE, lambda e: e.matmul(out=out.ap, lhsT=lhsT.ap, rhs=rhs.ap, start=start, stop=stop, **kw),
                  [lhsT.res, rhs.res], [out.res])

    def tr(self, out, in_, ident):
        self.S.op(PE, lambda e: e.transpose(out=out.ap, in_=in_.ap, identity=ident.ap),
                  [in_.res, ident.res], [out.res])

    def act(self, out, in_, func, bias=None, scale=1.0, accum=None, eng=ACT):
        rd = [in_.res]
        kw = {}
        if bias is not None:
            if isinstance(bias, V):
                rd.append(bias.res)
                kw["bias"] = bias.ap
            else:
                kw["bias"] = bias
        if isinstance(scale, V):
            rd.append(scale.res)
            kw["scale"] = scale.ap
        else:
            kw["scale"] = scale
        wr = [out.res]
        if accum is not None:
            wr.append(accum.res)
            kw["accum_out"] = accum.ap
        self.S.op(ACT, lambda e: e.activation(out=out.ap, in_=in_.ap, func=func, **kw), rd, wr)

    def _sc(self, s, rd):
        if isinstance(s, V):
            rd.append(s.res)
            return s.ap
        return s

    def ts(self, eng, out, in0, s1, s2, op0, op1=None, accum=None):
        rd = [in0.res]
        a1 = self._sc(s1, rd)
        a2 = self._sc(s2, rd)
        wr = [out.res]
        kw = {}
        if op1 is not None:
            kw["op1"] = op1
        if accum is not None:
            wr.append(accum.res)
            kw["accum_out"] = accum.ap
        self.S.op(eng, lambda e: e.tensor_scalar(out=out.ap, in0=in0.ap, scalar1=a1, scalar2=a2, op0=op0, **kw), rd, wr)

    def tt(self, eng, out, in0, in1, op):
        self.S.op(eng, lambda e: e.tensor_tensor(out=out.ap, in0=in0.ap, in1=in1.ap, op=op),
                  [in0.res, in1.res], [out.res])

    def stt(self, eng, out, in0, s, in1, op0, op1):
        rd = [in0.res, in1.res]
        a = self._sc(s, rd)
        self.S.op(eng, lambda e: e.scalar_tensor_tensor(out=out.ap, in0=in0.ap, scalar=a, in1=in1.ap, op0=op0, op1=op1),
                  rd, [out.res])

    def cp(self, eng, out, in_):
        if eng == ACT:
            self.S.op(ACT, lambda e: e.copy(out=out.ap, in_=in_.ap), [in_.res], [out.res])
        else:
            self.S.op(eng, lambda e: e.tensor_copy(out=out.ap, in_=in_.ap), [in_.res], [out.res])

    def memset(self, eng, out, val):
        self.S.op(eng, lambda e: e.memset(out.ap, val), [], [out.res])

    def recip(self, out, in_):
        self.S.op(DVE, lambda e: e.reciprocal(out=out.ap, in_=in_.ap), [in_.res], [out.res])

    def reduce(self, eng, out, in_, op, axis=AX.X):
        self.S.op(eng, lambda e: e.tensor_reduce(out=out.ap, in_=in_.ap, axis=axis, op=op), [in_.res], [out.res])

    def max8(self, out, in_):
        self.S.op(DVE, lambda e: e.max(out=out.ap, in_=in_.ap), [in_.res], [out.res])

    def match_replace(self, out, rep, vals, imm):
        self.S.op(DVE, lambda e: e.match_replace(out=out.ap, in_to_replace=rep.ap, in_values=vals.ap, imm_value=imm),
                  [rep.res, vals.res], [out.res])

S_LEN = 4096
DM = 1024
EPS = 1e-6
NEG = -30000.0
U_MLO = -127
U_W = 2400
T_NOFF = 2176
T_L = 4480
bf = ml_dtypes.bfloat16


def _bucket_np(n):
    n = np.maximum(n, 0)
    logv = np.log(np.maximum(n, 1).astype(np.float32) / np.float32(16)) / np.float32(np.log(128 / 16))
    large = 16 + (logv.astype(np.float32) * np.float32(16)).astype(np.int32)
    large = np.minimum(large, 31)
    return np.where(n < 16, n, large)


def host_consts():
    C = {}
    i = np.arange(128)
    same = (i[:, None] // 64) == (i[None, :] // 64)
    C["c_ident"] = np.eye(128, dtype=np.float32)
    C["c_tri"] = (same & (i[:, None] <= i[None, :])).astype(np.float32)
    C["c_sellast"] = (i[:, None] == (64 * (i[None, :] // 64) + 63)).astype(np.float32)
    C["c_sela"] = np.repeat((i == 63).astype(np.float32)[:, None], 128, 1)
    C["c_selb"] = np.repeat((i == 127).astype(np.float32)[:, None], 128, 1)
    C["c_maskneg"] = np.where(same & (i[:, None] >= i[None, :]), 0.0, -NEG).astype(np.float32)
    C["c_strict"] = (same & (i[:, None] > i[None, :])).astype(np.float32)
    n = np.arange(T_L) - T_NOFF
    oh = np.zeros((33, T_L), np.float32)
    b = _bucket_np(n)
    for t in range(T_L):
        if n[t] >= 0:
            oh[b[t], t] += 1.0
            oh[31, t] -= 1.0
        else:
            oh[32, t] = NEG
    C["c_ohx"] = oh
    C["c_m4t"] = np.where(i[None, :] < i[:, None], 0.0, NEG).astype(bf)
    C["c_e"] = (np.arange(64)[:, None] == (np.arange(4096)[None, :] // 64)).astype(bf)
    c_start = np.arange(255) * 16
    c_end = c_start + 31
    s_start = np.arange(64) * 64
    sm = ((c_start[:, None] < s_start[None, :] + 64) & (s_start[None, :] <= c_end[:, None])).astype(np.float32)
    smx = np.zeros((256, 65), np.float32)
    smx[:255, :64] = sm
    smx[:, 64] = 1.0
    C["c_selmap"] = smx.reshape(2, 128, 65).transpose(1, 0, 2).astype(bf).copy()
    C["c_d0"] = (np.arange(64)[None, :] - (i[:, None] >= 64)).astype(np.float32)
    C["c_jb100"] = np.repeat((100.0 + np.arange(64, dtype=np.float32))[None, :], 128, 0)
    sr = np.zeros((24, 24, 128), np.float32)
    for hb in range(24):
        sr[hb, hb, :] = 1.0
    C["c_selrow"] = sr
    return C


def host_layout(inp):
    L = {}
    col = lambda v: np.ascontiguousarray(v.reshape(-1, 128).T).astype(np.float32)
    rep = lambda v: np.ascontiguousarray(np.broadcast_to(v[None, :], (128, v.shape[0]))).astype(np.float32)
    for i in range(2):
        L["p_gmix%d" % i] = col(inp["norm_mix_w"][i])
        L["p_gffn%d" % i] = col(inp["norm_ffn_w"][i])
        L["p_gmem%d" % i] = col(inp["mem_norm_w"][i])
        L["p_fcw%d" % i] = np.ascontiguousarray(inp["ffn_conv_w"][i].reshape(3, 22, 128).transpose(2, 0, 1))
        L["p_fcb%d" % i] = col(inp["ffn_conv_b"][i])
    L["p_fw"] = rep(inp["final_norm_w"])
    L["p_gcw"] = np.ascontiguousarray(inp["gdn_conv_w"][0].reshape(4, 24, 128).transpose(2, 1, 0))
    L["p_alog"] = rep(inp["gdn_a_log"][0])
    L["p_dtb"] = rep(inp["gdn_dt_bias"][0])
    L["p_gnw"] = rep(inp["gdn_norm_w"][0])
    L["p_posk"] = np.ascontiguousarray(inp["nsa_cmp_pos_k"][0].T)
    L["p_posv"] = np.ascontiguousarray(inp["nsa_cmp_pos_v"][0].T)
    rbx = np.ones((33, 8), np.float32)
    rbx[:32] = inp["rel_bias"]
    L["p_rbx"] = rbx
    return L


BIG_INPUTS = {
    "x": (S_LEN, DM), "mem": (256, DM), "mem_w_kv": (2, 1024, 1024), "w_out": (2, 1536, 1024),
    "gdn_w_in": (1024, 4624), "nsa_w_in": (1024, 3096), "ffn_w_up": (2, 1024, 5632),
    "ffn_w_down": (2, 2816, 1024), "w1k": (4096, 128), "w2k": (128, 128), "w1v": (4096, 128), "w2v": (128, 128),
}


def rms_to_hT(kb, W, xt, gcol, hT, c0):
    kb.memset(DVE, W["ss"][:], 0.0)
    kb.act(W["junk"][:], xt, AF.Square, accum=W["ss"][:])
    kb.act(W["rstd"][:], W["ss"][:], AF.Sqrt, scale=1.0 / DM, bias=W["epsc"][:, 0:1])
    kb.recip(W["rstd"][:], W["rstd"][:])
    kb.ts(DVE, W["xn"][:], xt, W["rstd"][:, 0:1], None, ALU.mult)
    for k in range(8):
        kb.tr(W["ptr"][:, k * 128:(k + 1) * 128], W["xn"][:, k * 128:(k + 1) * 128], W["identb"][:])
    kb.tt(DVE, hT[:, :, c0:c0 + 128], W["ptr"][:].m(lambda a: a.rearrange("p (k t) -> p k t", k=8)),
          gcol[:].m(lambda a: a.unsqueeze(2).to_broadcast([128, 8, 128])), ALU.mult)


def common_work(kb, T):
    W = {}
    W["ident"] = kb.sb("ident", [128, 128])
    kb.dma(SP, W["ident"][:], T["c_ident"])
    W["identb"] = kb.sb("identb", [128, 128], BF16)
    kb.cp(DVE, W["identb"][:], W["ident"][:])
    W["ones"] = kb.sb("ones", [128, 128])
    kb.memset(POOL, W["ones"][:], 1.0)
    W["onesb"] = kb.sb("onesb", [128, 128], BF16)
    kb.memset(POOL, W["onesb"][:], 1.0)
    W["epsc"] = kb.sb("epsc", [128, 1])
    kb.memset(POOL, W["epsc"][:], EPS)
    W["ss"] = kb.sb("ss", [128, 1])
    W["rstd"] = kb.sb("rstd", [128, 1])
    W["junk"] = kb.sb("junk", [128, 1024])
    W["xn"] = kb.sb("xn", [128, 1024], BF16)
    W["ptr"] = kb.ps("ptr", [128, 1024], BF16)
    return W


def load_w_bf16(kb, name, dram2d, nk, ncol):
    w = kb.sb(name, [128, nk, ncol], BF16)
    for k in range(nk):
        kb.dma(POOL, w[:, k, :], dram2d[k * 128:(k + 1) * 128, :])
    return w


def mem_prep(kb, T, W, i, PA, PB):
    wkv = load_w_bf16(kb, "wkv", T["mem_w_kv"][i], 8, 1024)
    gcm = kb.sb("gcm", [128, 8])
    kb.dma(SP, gcm[:], T["p_gmem%d" % i])
    memnT = kb.sb("memnT", [128, 8, 256], BF16)
    for a in range(2):
        xt = kb.sb("memx%d" % a, [128, 1024])
        kb.dma(SP, xt[:], T["mem"][a * 128:(a + 1) * 128, :])
        rms_to_hT(kb, W, xt[:], gcm, memnT, a * 128)
    kmemT = kb.sb("kmemT", [128, 4, 256], BF16)
    vmem = kb.sb("vmem", [128, 2, 512], BF16)
    for h in range(4):
        for k in range(8):
            kb.mm(PA[:, 0:256], wkv[:, k, h * 128:(h + 1) * 128], memnT[:, k, :], start=(k == 0), stop=(k == 7))
        kb.cp(ACT, kmemT[:, h, :], PA[:, 0:256])
    for mc in range(2):
        for k in range(8):
            kb.mm(PB[:], memnT[:, k, mc * 128:(mc + 1) * 128], wkv[:, k, 512:1024], start=(k == 0), stop=(k == 7))
        kb.cp(DVE, vmem[:, mc, :], PB[:])
    return kmemT, vmem


def mem_attn(kb, W, qmT, kmemT, vmem, moTst, PS0, PS1, PO, PD, pTs, rden):
    for h in range(4):
        for mc, ps in ((0, PS0), (1, PS1)):
            kb.mm(ps[:], kmemT[:, h, mc * 128:(mc + 1) * 128], qmT[:, h, :])
            kb.act(pTs[mc][:], ps[:], AF.Exp)
        for mc in range(2):
            kb.mm(PO[:], vmem[:, mc, h * 128:(h + 1) * 128], pTs[mc][:], start=(mc == 0), stop=(mc == 1))
        for mc in range(2):
            kb.mm(PD[:], W["onesb"][:], pTs[mc][:], start=(mc == 0), stop=(mc == 1))
        kb.recip(rden[:], PD[:])
        kb.tt(DVE, moTst[:, h, :], PO[:], rden[:], ALU.mult)


def phase_l0_in(kb, T):
    with kb.phase("l0in"):
        W = common_work(kb, T)
        PA = [kb.ps("PA%d" % i) for i in range(2)]
        PB = kb.ps("PB")
        PT = kb.ps("PT")
        PS0, PS1, PO = kb.ps("PS0"), kb.ps("PS1"), kb.ps("PO")
        win = load_w_bf16(kb, "win", T["gdn_w_in"], 8, 4624)
        gcol = kb.sb("gcol", [128, 8])
        kb.dma(SP, gcol[:], T["p_gmix0"])
        cwT = kb.sb("cwT", [128, 24, 4])
        kb.dma(SP, cwT[:], T["p_gcw"])
        alog = kb.sb("alog", [128, 8])
        kb.dma(SP, alog[:], T["p_alog"])
        dtb = kb.sb("dtb", [128, 8])
        kb.dma(SP, dtb[:], T["p_dtb"])
        negA = kb.sb("negA", [128, 8])
        kb.act(negA[:], alog[:], AF.Exp)
        kb.ts(DVE, negA[:], negA[:], -1.0, None, ALU.mult)
        kmemT, vmem = mem_prep(kb, T, W, 0, PA[0], PA[1])
        halo = kb.sb("halo", [128, 24, 3])
        kb.memset(POOL, halo[:], 0.0)
        hTs = [kb.sb("hT%d" % i, [128, 8, 512], BF16) for i in range(2)]
        xts = [kb.sb("xt%d" % i, [128, 1024]) for i in range(2)]
        pres = [kb.sb("pre%d" % i, [128, 515]) for i in range(2)]
        accs = [kb.sb("acc%d" % i, [128, 512]) for i in range(2)]
        ys = [kb.sb("y%d" % i, [128, 512]) for i in range(2)]
        sq = kb.sb("sq", [128, 512])
        rn = kb.sb("rn", [128, 512])
        stg = [kb.sb("stg%d" % i, [128, 4, 128]) for i in range(2)]
        zts = [kb.sb("zt%d" % i, [128, 1024]) for i in range(2)]
        bgt = [kb.sb("bgt%d" % i, [128, 16]) for i in range(2)]
        xx = kb.sb("xx", [128, 8])
        qmT = kb.sb("qmT", [128, 4, 512], BF16)
        pTs = [kb.sb("pTs%d" % i, [128, 512], BF16) for i in range(2)]
        rden = kb.sb("rden", [128, 512])
        moTst = kb.sb("moTst", [128, 4, 512], BF16)
        for st in range(8):
            hT = hTs[st % 2]
            cs = slice(st * 512, (st + 1) * 512)
            for a in range(4):
                xt = xts[a % 2]
                kb.dma(SP if a % 2 == 0 else ACT, xt[:], T["x"][st * 512 + a * 128: st * 512 + (a + 1) * 128, :])
                rms_to_hT(kb, W, xt[:], gcol, hT, a * 128)
            for c in range(24):
                pa = PA[c % 2]
                for k in range(8):
                    kb.mm(pa[:], win[:, k, c * 128:(c + 1) * 128], hT[:, k, :], start=(k == 0), stop=(k == 7))
                pre, acc, y = pres[c % 2], accs[c % 2], ys[c % 2]
                kb.cp(POOL, pre[:, 0:3], halo[:, c, :])
                kb.cp(ACT, pre[:, 3:515], pa[:])
                kb.cp(POOL, halo[:, c, :], pre[:, 512:515])
                kb.ts(DVE, acc[:], pre[:, 3:515], cwT[:, c, 3:4], None, ALU.mult)
                kb.stt(DVE, acc[:], pre[:, 2:514], cwT[:, c, 2:3], acc[:], ALU.mult, ALU.add)
                kb.stt(DVE, acc[:], pre[:, 1:513], cwT[:, c, 1:2], acc[:], ALU.mult, ALU.add)
                kb.stt(DVE, acc[:], pre[:, 0:512], cwT[:, c, 0:1], acc[:], ALU.mult, ALU.add)
                kb.act(y[:], acc[:], AF.Silu)
                if c < 16:
                    kb.tt(POOL, sq[:], y[:], y[:], ALU.mult)
                    kb.mm(PB[:], W["ones"][:], sq[:])
                    sc_ = 128.0 if c < 8 else 1.0
                    kb.act(rn[:], PB[:], AF.Sqrt, scale=sc_, bias=W["epsc"][:, 0:1])
                    kb.recip(rn[:], rn[:])
                    kb.tt(DVE, y[:], y[:], rn[:], ALU.mult)
                if c < 8:
                    kb.dma(SP, T["qT"][c, :, cs], y[:])
                else:
                    if c < 16:
                        kb.dma(SP, T["kT"][c - 8, :, cs], y[:])
                    for a in range(4):
                        kb.tr(PT[:, a * 128:(a + 1) * 128], y[:, a * 128:(a + 1) * 128], W["ident"][:])
                    sg = stg[c % 2]
                    kb.cp(ACT, sg[:], PT[:].m(lambda ap: ap.rearrange("p (a f) -> p a f", a=4)))
                    dst = T["kTM"] if c < 16 else T["vTM"]
                    hh = (c - 8) % 8
                    kb.dma(ACT, dst[cs, hh * 128:(hh + 1) * 128].rearrange("(a p) f -> p a f", p=128), sg[:])
            for h4 in range(4):
                pa = PA[h4 % 2]
                for k in range(8):
                    kb.mm(pa[:], win[:, k, 4112 + h4 * 128: 4112 + (h4 + 1) * 128], hT[:, k, :], start=(k == 0), stop=(k == 7))
                kb.ts(DVE, qmT[:, h4, :], pa[:], 128.0 ** -0.5, None, ALU.mult)
            mem_attn(kb, W, qmT, kmemT, vmem, moTst, PS0, PS1, PO, PB, pTs, rden)
            kb.dma(SP, T["moT"][:, :, cs].rearrange("h p t -> p h t"), moTst[:])
            for a in range(4):
                ts_ = slice(a * 128, (a + 1) * 128)
                rows = slice(st * 512 + a * 128, st * 512 + (a + 1) * 128)
                zt = zts[a % 2]
                for n in range(2):
                    for k in range(8):
                        kb.mm(PA[n][:], hT[:, k, ts_], win[:, k, 3072 + n * 512: 3072 + (n + 1) * 512], start=(k == 0), stop=(k == 7))
                    kb.act(zt[:, n * 512:(n + 1) * 512], PA[n][:], AF.Silu)
                kb.dma(SP, T["zs"][rows, :], zt[:])
                for k in range(8):
                    kb.mm(PT[:, 0:16], hT[:, k, ts_], win[:, k, 4096:4112], start=(k == 0), stop=(k == 7))
                bg = bgt[a % 2]
                kb.act(bg[:, 0:8], PT[:, 0:8], AF.Sigmoid)
                kb.tt(DVE, xx[:], PT[:, 8:16], dtb[:], ALU.add)
                kb.act(xx[:], xx[:], AF.Exp)
                kb.act(xx[:], xx[:], AF.Ln, bias=1.0)
                kb.tt(DVE, bg[:, 8:16], xx[:], negA[:], ALU.mult)
                kb.dma(ACT, T["bg"][rows, :], bg[:])


def out_proj(kb, wout, mixT, mot, xt, xo, P2, dst_rows):
    for n in range(2):
        o_ = P2[:, n * 512:(n + 1) * 512]
        for fc in range(8):
            kb.mm(o_, mixT[:, fc, :], wout[:, fc, n * 512:(n + 1) * 512], start=(fc == 0), stop=False)
        for h4 in range(4):
            kb.mm(o_, mot[:, h4, :], wout[:, 8 + h4, n * 512:(n + 1) * 512], start=False, stop=(h4 == 3))
    kb.tt(DVE, xo[:], xt[:], P2[:], ALU.add)
    kb.dma(SP, dst_rows, xo[:])


def phase_l0_gdn(kb, T):
    with kb.phase("l0gdn"):
        ident = kb.sb("ident", [128, 128])
        kb.dma(SP, ident[:], T["c_ident"])
        cn = {}
        for nm in ("tri", "sellast", "sela", "selb", "maskneg", "strict"):
            cn[nm] = kb.sb(nm, [128, 128])
            kb.dma(SP, cn[nm][:], T["c_" + nm])
        ones = kb.sb("ones", [128, 128])
        kb.memset(POOL, ones[:], 1.0)
        epsc = kb.sb("epsc", [128, 1])
        kb.memset(POOL, epsc[:], EPS)
        wout = load_w_bf16(kb, "wout", T["w_out"][0], 12, 1024)
        nw = kb.sb("nw", [128, 128])
        kb.dma(SP, nw[:], T["p_gnw"])
        S_ = kb.sb("S", [128, 8, 128])
        kb.memset(POOL, S_[:], 0.0)
        P = [kb.ps("P%d" % i, [128, 1024]) for i in range(4)]
        v3 = lambda t: t[:].m(lambda a: a.rearrange("p (h j) -> p h j", h=8))
        P3 = [v3(p) for p in P]
        bc8 = lambda v: v.m(lambda a: a.unsqueeze(1).to_broadcast([128, 8, 128]))
        sc8 = lambda v: v.m(lambda a: a.unsqueeze(2).to_broadcast([128, 8, 128]))
        NB = 2
        qTt = [kb.sb("qTt%d" % i, [128, 8, 128]) for i in range(NB)]
        kTt = [kb.sb("kTt%d" % i, [128, 8, 128]) for i in range(NB)]
        kTMt = [kb.sb("kTMt%d" % i, [128, 8, 128]) for i in range(NB)]
        vTMt = [kb.sb("vTMt%d" % i, [128, 8, 128]) for i in range(NB)]
        kf = [kb.sb("kf%d" % i, [64, 2, 8, 128]) for i in range(NB)]
        zf = [kb.sb("zf%d" % i, [64, 2, 1024]) for i in range(1)] * NB
        bgts = [kb.sb("bgt%d" % i, [128, 16]) for i in range(NB)]
        xts = [kb.sb("xt%d" % i, [128, 1024]) for i in range(NB)]
        mots = [kb.sb("mot%d" % i, [128, 4, 128], BF16) for i in range(NB)]
        gsm = kb.sb("gsm", [128, 8])
        sm = kb.sb("sm", [128, 64])
        expgc = kb.sb("expgc", [128, 8])
        egl = kb.sb("egl", [128, 2, 8])
        egf = kb.sb("egf", [64, 2, 8])
        dff = kb.sb("dff", [64, 2, 8])
        kdff = kb.sb("kdff", [64, 2, 8])
        bexp = kb.sb("bexp", [128, 8])
        big = lambda nm: kb.sb(nm, [128, 8, 128])
        dg, tmp, Dm, Am = big("dg"), big("tmp"), big("Dm"), big("Am")
        Xs = [big("Xa"), big("Xb")]
        Ys = [big("Ya"), big("Yb")]
        Rs = [big("Ra"), big("Rb")]
        vb, kbg, wT = big("vb"), big("kbg"), big("wT")
        ATf = kb.sb("ATf", [64, 2, 8, 64])
        kd_f = kb.sb("kd_f", [64, 2, 8, 128])
        u = kb.sb("u", [64, 2, 8, 128])
        vnew = kb.sb("vnew", [64, 8, 128])
        t1 = kb.sb("t1", [64, 8, 128])
        o_f = kb.sb("o_f", [64, 2, 8, 128])
        sqo = kb.sb("sqo", [64, 2, 8, 128])
        ssq = kb.sb("ssq", [64, 16])
        mixT = kb.sb("mixT", [128, 8, 128], BF16)
        xo = kb.sb("xo", [128, 1024])
        for tt_ in range(32):
            b_ = tt_ % NB
            cs = slice(tt_ * 128, (tt_ + 1) * 128)
            kb.dma(SP, qTt[b_][:], T["qT"][:, :, cs].rearrange("h d t -> d h t"))
            kb.dma(ACT, kTt[b_][:], T["kT"][:, :, cs].rearrange("h d t -> d h t"))
            kb.dma(SP, kTMt[b_][:], T["kTM"][cs, :].rearrange("p (h f) -> p h f", h=8))
            kb.dma(ACT, vTMt[b_][:], T["vTM"][cs, :].rearrange("p (h f) -> p h f", h=8))
            kb.dma(SP, kf[b_][:], T["kTM"][cs, :].rearrange("(c p) (h f) -> p c h f", p=64, h=8))
            kb.dma(ACT, zf[b_][:], T["zs"][cs, :].rearrange("(c p) f -> p c f", p=64))
            kb.dma(SP, bgts[b_][:], T["bg"][cs, :])
            kb.dma(ACT, xts[b_][:], T["x"][cs, :])
            kb.dma(SP, mots[b_][:], T["moT"][:, :, cs].rearrange("h p t -> p h t"))
            bg = bgts[b_]
            beta, g = bg[:, 0:8], bg[:, 8:16]
            q_, k_ = qTt[b_], kTt[b_]
            ps = P[0]
            kb.mm(ps[:, 0:8], cn["tri"][:], g)
            kb.cp(DVE, gsm[:], ps[:, 0:8])
            kb.mm(ps[:, 8:16], cn["sellast"][:], gsm[:])
            kb.mm(ps[:, 16:24], cn["sela"][:], gsm[:])
            kb.mm(ps[:, 24:32], cn["selb"][:], gsm[:])
            for c in range(2):
                kb.mm(ps[0:64, 32 + c * 8: 40 + c * 8], cn["tri"][:, c * 64:(c + 1) * 64], g)
                kb.mm(ps[0:64, 48 + c * 8: 56 + c * 8], cn["sellast"][:, c * 64:(c + 1) * 64], gsm[:])
            kb.cp(DVE, sm[:, 8:32], ps[:, 8:32])
            kb.cp(DVE, sm[0:64, 32:64], ps[0:64, 32:64])
            kb.act(expgc[:], gsm[:], AF.Exp)
            kb.act(egl[:].m(lambda a: a.rearrange("p c h -> p (c h)")), sm[:, 16:32], AF.Exp)
            kb.act(egf[:].m(lambda a: a.rearrange("p c h -> p (c h)")), sm[0:64, 32:48], AF.Exp)
            kb.tt(DVE, dff[:].m(lambda a: a.rearrange("p c h -> p (c h)")), sm[0:64, 48:64], sm[0:64, 32:48], ALU.subtract)
            kb.act(kdff[:], dff[:], AF.Exp)
            kb.tt(DVE, bexp[:], beta, expgc[:], ALU.mult)
            kb.tt(DVE, dg[:], bc8(ident[:]), sc8(gsm[:]), ALU.mult)
            for h in range(8):
                kb.mm(P3[1][:, h, :], ones[:], dg[:, h, :])
            kb.tt(DVE, tmp[:], P3[1], bc8(cn["maskneg"][:]), ALU.add)
            for h in range(8):
                kb.act(Dm[:, h, :], tmp[:, h, :], AF.Exp, bias=gsm[:, h:h + 1], scale=-1.0)
            for h in range(8):
                kb.mm(P3[2][:, h, :], k_[:, h, :], k_[:, h, :])
            for h in range(8):
                kb.mm(P3[3][:, h, :], q_[:, h, :], k_[:, h, :])
            X0, Y0, R0 = Xs[0], Ys[0], Rs[0]
            kb.tt(DVE, X0[:], P3[2], Dm[:], ALU.mult)
            kb.tt(POOL, X0[:], X0[:], bc8(cn["strict"][:]), ALU.mult)
            kb.tt(DVE, X0[:], X0[:], sc8(beta), ALU.mult)
            kb.tt(DVE, Am[:], P3[3], Dm[:], ALU.mult)
            for h in range(8):
                kb.tr(P3[0][:, h, :], X0[:, h, :], ident[:])
            kb.cp(ACT, Y0[:], P3[0])
            for c in range(2):
                for h in range(8):
                    kb.tr(P[1][0:64, (c * 8 + h) * 64:(c * 8 + h + 1) * 64],
                          Am[c * 64:(c + 1) * 64, h, c * 64:(c + 1) * 64], ident[c * 64:(c + 1) * 64, c * 64:(c + 1) * 64])
            kb.cp(DVE, ATf[:].m(lambda a: a.rearrange("p c h i -> p (c h i)")), P[1][0:64, :])
            kb.tt(DVE, R0[:], bc8(ident[:]), Y0[:], ALU.subtract)
            Xp, Yp, Rp = X0, Y0, R0
            for l in range(1, 6):
                Xn, Yn, Rn = Xs[l % 2], Ys[l % 2], Rs[l % 2]
                for h in range(8):
                    kb.mm(P3[2][:, h, :], Yp[:, h, :], Xp[:, h, :])
                if l < 5:
                    for h in range(8):
                        kb.mm(P3[3][:, h, :], Xp[:, h, :], Yp[:, h, :])
                kb.cp(ACT, Xn[:], P3[2])
                if l < 5:
                    kb.cp(DVE, Yn[:], P3[3])
                for h in range(8):
                    kb.mm(P3[0][:, h, :], Xn[:, h, :], Rp[:, h, :])
                kb.tt(DVE, Rn[:], Rp[:], P3[0], ALU.add)
                Xp, Yp, Rp = Xn, Yn, Rn
            R = Rp
            kb.tt(POOL, vb[:], vTMt[b_][:], sc8(beta), ALU.mult)
            kb.tt(POOL, kbg[:], kTMt[b_][:], sc8(bexp[:]), ALU.mult)
            kb.tt(POOL, kd_f[:], kf[b_][:], kdff[:].m(lambda a: a.unsqueeze(3).to_broadcast([64, 2, 8, 128])), ALU.mult)
            for c in range(2):
                for h in range(8):
                    kb.mm(P3[2][0:64, h, :], R[:, h, c * 64:(c + 1) * 64], vb[:, h, :])
                kb.cp(ACT, u[:, c, :, :], P3[2][0:64, :, :])
            for h in range(8):
                kb.mm(P3[3][:, h, :], kbg[:, h, :], R[:, h, :])
            kb.cp(DVE, wT[:], P3[3])
            for c in range(2):
                cc = slice(c * 64, (c + 1) * 64)
                for h in range(8):
                    kb.mm(P3[0][0:64, h, :], wT[:, h, cc], S_[:, h, :])
                for h in range(8):
                    kb.mm(P3[1][0:64, h, :], q_[:, h, cc], S_[:, h, :])
                kb.tt(DVE, vnew[:], u[:, c, :, :], P3[0][0:64, :, :], ALU.subtract)
                for h in range(8):
                    kb.mm(P3[2][0:64, h, :], ATf[:, c, h, :], vnew[:, h, :])
                kb.tt(DVE, t1[:], P3[1][0:64, :, :], egf[:, c, :].m(lambda a: a.unsqueeze(2).to_broadcast([64, 8, 128])), ALU.mult)
                kb.tt(DVE, o_f[:, c, :, :], t1[:], P3[2][0:64, :, :], ALU.add)
                for h in range(8):
                    kb.mm(P3[3][:, h, :], kd_f[:, c, h, :], vnew[:, h, :])
                kb.tt(POOL, S_[:], S_[:], sc8(egl[:, c, :]), ALU.mult)
                kb.tt(DVE, S_[:], S_[:], P3[3], ALU.add)
            o16 = o_f[:].m(lambda a: a.rearrange("p c h e -> p (c h) e"))
            kb.tt(POOL, sqo[:], o_f[:], o_f[:], ALU.mult)
            kb.reduce(DVE, ssq[:], sqo[:].m(lambda a: a.rearrange("p c h e -> p (c h) e")), ALU.add)
            kb.act(ssq[:], ssq[:], AF.Sqrt, scale=1.0 / 128, bias=epsc[0:64, 0:1])
            kb.recip(ssq[:], ssq[:])
            kb.tt(DVE, o16, o16, ssq[:].m(lambda a: a.unsqueeze(2).to_broadcast([64, 16, 128])), ALU.mult)
            kb.tt(POOL, o16, o16, nw[0:64, :].m(lambda a: a.unsqueeze(1).to_broadcast([64, 16, 128])), ALU.mult)
            of2 = o_f[:].m(lambda a: a.rearrange("p c h e -> p c (h e)"))
            kb.tt(DVE, of2, of2, zf[b_][:], ALU.mult)
            for c in range(2):
                for fc in range(8):
                    kb.tr(P3[0][:, fc, c * 64:(c + 1) * 64], o_f[:, c, fc, :], ident[0:64, 0:64])
            kb.cp(ACT, mixT[:], P3[0])
            out_proj(kb, wout, mixT, mots[b_], xts[b_], xo, P[1], T["xa"][cs, :])


def phase_ffn(kb, T, i, src, dst, final):
    with kb.phase("ffn%d" % i):
        W = common_work(kb, T)
        wup = load_w_bf16(kb, "wup", T["ffn_w_up"][i], 8, 5632)
        wdn = load_w_bf16(kb, "wdn", T["ffn_w_down"][i], 22, 1024)
        gcol = kb.sb("gcol", [128, 8])
        kb.dma(SP, gcol[:], T["p_gffn%d" % i])
        cw = kb.sb("cw", [128, 3, 22])
        kb.dma(SP, cw[:], T["p_fcw%d" % i])
        cb = kb.sb("cb", [128, 22])
        kb.dma(SP, cb[:], T["p_fcb%d" % i])
        if final:
            fw = kb.sb("fw", [128, 1024])
            kb.dma(SP, fw[:], T["p_fw"])
        halo = kb.sb("halo", [128, 22, 2])
        kb.memset(POOL, halo[:], 0.0)
        PG = [kb.ps("PG%d" % j) for j in range(2)]
        PV = [kb.ps("PV%d" % j) for j in range(2)]
        PO = [kb.ps("PO%d" % j) for j in range(2)]
        ST = 256
        hTs = [kb.sb("hT%d" % j, [128, 8, ST], BF16) for j in range(2)]
        xts = [kb.sb("xt%d" % j, [128, 1024]) for j in range(4)]
        gbs = [kb.sb("gb%d" % j, [128, ST + 2]) for j in range(2)]
        accs = [kb.sb("acc%d" % j, [128, ST]) for j in range(2)]
        sgs = [kb.sb("sg%d" % j, [128, ST]) for j in range(2)]
        actT = kb.sb("actT", [128, 22, ST], BF16)
        xos = [kb.sb("xo%d" % j, [128, 1024]) for j in range(2)]
        for st in range(S_LEN // ST):
            hT = hTs[st % 2]
            for a in range(2):
                xt = xts[(st % 2) * 2 + a]
                r0 = st * ST + a * 128
                kb.dma(SP if a == 0 else ACT, xt[:], src[r0:r0 + 128, :])
                rms_to_hT(kb, W, xt[:], gcol, hT, a * 128)
            for c in range(22):
                pg, pv = PG[c % 2], PV[c % 2]
                for k in range(8):
                    kb.mm(pg[:, 0:ST], wup[:, k, c * 128:(c + 1) * 128], hT[:, k, :], start=(k == 0), stop=(k == 7))
                for k in range(8):
                    kb.mm(pv[:, 0:ST], wup[:, k, 2816 + c * 128: 2816 + (c + 1) * 128], hT[:, k, :], start=(k == 0), stop=(k == 7))
                gb, acc, sg = gbs[c % 2], accs[c % 2], sgs[c % 2]
                kb.cp(POOL, gb[:, 0:2], halo[:, c, :])
                kb.cp(ACT, gb[:, 2:ST + 2], pg[:, 0:ST])
                kb.cp(POOL, halo[:, c, :], gb[:, ST:ST + 2])
                kb.ts(DVE, acc[:], gb[:, 2:ST + 2], cw[:, 2, c:c + 1], cb[:, c:c + 1], ALU.mult, ALU.add)
                kb.stt(DVE, acc[:], gb[:, 1:ST + 1], cw[:, 1, c:c + 1], acc[:], ALU.mult, ALU.add)
                kb.stt(DVE, acc[:], gb[:, 0:ST], cw[:, 0, c:c + 1], acc[:], ALU.mult, ALU.add)
                kb.act(sg[:], acc[:], AF.Silu)
                kb.tt(DVE, actT[:, c, :], sg[:], pv[:, 0:ST], ALU.mult)
            for a in range(2):
                xt = xts[(st % 2) * 2 + a]
                xo = xos[a]
                r0 = st * ST + a * 128
                for n in range(2):
                    for c in range(22):
                        kb.mm(PO[n][:], actT[:, c, a * 128:(a + 1) * 128], wdn[:, c, n * 512:(n + 1) * 512], start=(c == 0), stop=(c == 21))
                    kb.tt(DVE, xo[:, n * 512:(n + 1) * 512], xt[:, n * 512:(n + 1) * 512], PO[n][:], ALU.add)
                if final:
                    kb.memset(DVE, W["ss"][:], 0.0)
                    kb.act(W["junk"][:], xo[:], AF.Square, accum=W["ss"][:])
                    kb.act(W["rstd"][:], W["ss"][:], AF.Sqrt, scale=1.0 / DM, bias=W["epsc"][:, 0:1])
                    kb.recip(W["rstd"][:], W["rstd"][:])
                    kb.ts(DVE, xo[:], xo[:], W["rstd"][:, 0:1], None, ALU.mult)
                    kb.tt(POOL, xo[:], xo[:], fw[:], ALU.mult)
                kb.dma(SP, dst[r0:r0 + 128, :], xo[:])


def phase_l1_in(kb, T):
    with kb.phase("l1in"):
        W = common_work(kb, T)
        PA = [kb.ps("PA%d" % i) for i in range(2)]
        PB = kb.ps("PB")
        PS0, PS1, PO = kb.ps("PS0"), kb.ps("PS1"), kb.ps("PO")
        win = load_w_bf16(kb, "win", T["nsa_w_in"], 8, 3096)
        gcol = kb.sb("gcol", [128, 8])
        kb.dma(SP, gcol[:], T["p_gmix1"])
        kmemT, vmem = mem_prep(kb, T, W, 1, PA[0], PA[1])
        rbx = kb.sb("rbx", [33, 8])
        kb.dma(SP, rbx[:], T["p_rbx"])
        ohx = kb.sb("ohx", [33, T_L])
        kb.dma(SP, ohx[:], T["c_ohx"])
        tbl = kb.sb("tbl", [8, T_L], BF16)
        for j in range(0, T_L, 512):
            w_ = min(512, T_L - j)
            kb.mm(PB[0:8, 0:w_], rbx[:], ohx[:, j:j + w_])
            kb.cp(DVE, tbl[:, j:j + w_], PB[0:8, 0:w_])
        kb.dma(SP, T["tblD"], tbl[:])
        hTs = [kb.sb("hT%d" % i, [128, 8, 512], BF16) for i in range(2)]
        xts = [kb.sb("xt%d" % i, [128, 1024]) for i in range(2)]
        stg = [kb.sb("stg%d" % i, [128, 512], BF16) for i in range(2)]
        gst = kb.sb("gst", [24, 512])
        vst = [kb.sb("vst%d" % i, [128, 512], BF16) for i in range(2)]
        qmT = kb.sb("qmT", [128, 4, 512], BF16)
        pTs = [kb.sb("pTs%d" % i, [128, 512], BF16) for i in range(2)]
        rden = kb.sb("rden", [128, 512])
        moTst = kb.sb("moTst", [128, 4, 512], BF16)
        fm = [(h * 128, T["q1T"][h], 128.0 ** -0.5) for h in range(8)]
        for g in range(2):
            fm.append((1024 + g * 128, T["kcT"][g], 1.0))
            fm.append((1280 + g * 128, T["vcT"][g], 1.0))
            fm.append((1536 + g * 128, T["ksT"][g], 1.0))
            fm.append((2048 + g * 128, T["kwT"][g], 1.0))
        for st in range(8):
            hT = hTs[st % 2]
            cs = slice(st * 512, (st + 1) * 512)
            for a in range(4):
                xt = xts[a % 2]
                kb.dma(SP if a % 2 == 0 else ACT, xt[:], T["xb"][st * 512 + a * 128: st * 512 + (a + 1) * 128, :])
                rms_to_hT(kb, W, xt[:], gcol, hT, a * 128)
            for ci, (c0, dst, scl) in enumerate(fm):
                pa = PA[ci % 2]
                for k in range(8):
                    kb.mm(pa[:], win[:, k, c0:c0 + 128], hT[:, k, :], start=(k == 0), stop=(k == 7))
                sg = stg[ci % 2]
                if ci % 2 == 0:
                    kb.ts(DVE, sg[:], pa[:], scl, None, ALU.mult)
                else:
                    kb.act(sg[:], pa[:], AF.Copy, scale=scl)
                kb.dma(SP if ci % 2 == 0 else ACT, dst[:, cs], sg[:])
            for k in range(8):
                kb.mm(PB[0:24, :], win[:, k, 2560:2584], hT[:, k, :], start=(k == 0), stop=(k == 7))
            kb.act(gst[:], PB[0:24, :], AF.Sigmoid)
            kb.dma(SP, T["gT"][:, cs], gst[:])
            for h4 in range(4):
                pa = PA[h4 % 2]
                for k in range(8):
                    kb.mm(pa[:], win[:, k, 2584 + h4 * 128: 2584 + (h4 + 1) * 128], hT[:, k, :], start=(k == 0), stop=(k == 7))
                kb.ts(DVE, qmT[:, h4, :], pa[:], 128.0 ** -0.5, None, ALU.mult)
            mem_attn(kb, W, qmT, kmemT, vmem, moTst, PS0, PS1, PO, PB, pTs, rden)
            kb.dma(SP, T["moT"][:, :, cs].rearrange("h p t -> p h t"), moTst[:])
            for a in range(4):
                ts_ = slice(a * 128, (a + 1) * 128)
                rows = slice(st * 512 + a * 128, st * 512 + (a + 1) * 128)
                pa = PA[a % 2]
                for k in range(8):
                    kb.mm(pa[:, 0:256], hT[:, k, ts_], win[:, k, 1792:2048], start=(k == 0), stop=(k == 7))
                for k in range(8):
                    kb.mm(pa[:, 256:512], hT[:, k, ts_], win[:, k, 2304:2560], start=(k == 0), stop=(k == 7))
                vs_ = vst[a % 2]
                kb.cp(ACT, vs_[:], pa[:])
                kb.dma(SP, T["vs"][rows, :], vs_[:, 0:256])
                kb.dma(ACT, T["vw"][rows, :], vs_[:, 256:512])


def phase_l1_attn(kb, T):
    with kb.phase("l1attn"):
        ident = kb.sb("ident", [128, 128])
        kb.dma(SP, ident[:], T["c_ident"])
        identb = kb.sb("identb", [128, 128], BF16)
        kb.cp(DVE, identb[:], ident[:])
        onesb = kb.sb("onesb", [128, 128], BF16)
        kb.memset(POOL, onesb[:], 1.0)
        B = [kb.ps("B%d" % i) for i in range(7)]
        PTB = kb.ps("PTB", [128, 1024], BF16)
        wout = load_w_bf16(kb, "wout", T["w_out"][1], 12, 1024)
        m4t = kb.sb("m4t", [128, 128], BF16)
        kb.dma(SP, m4t[:], T["c_m4t"])
        E = kb.sb("E", [64, 4096], BF16)
        kb.dma(SP, E[:], T["c_e"])
        selmap = kb.sb("selmap", [128, 2, 65], BF16)
        kb.dma(SP, selmap[:], T["c_selmap"])
        d0 = kb.sb("d0", [128, 64])
        kb.dma(SP, d0[:], T["c_d0"])
        jb100 = kb.sb("jb100", [128, 64])
        kb.dma(SP, jb100[:], T["c_jb100"])
        selrow = kb.sb("selrow", [24, 24, 128])
        kb.dma(SP, selrow[:], T["c_selrow"])
        U = kb.sb("U", [128, 8, U_W], BF16)
        T01 = kb.sb("T01", [128, 8, 256], BF16)
        for ci in range(128):
            o_ = T_NOFF + U_MLO - 16 * ci
            kb.dma(SP if ci % 2 == 0 else ACT, U[ci:ci + 1, :, :], T["tblD"][:, o_:o_ + U_W])
            o2 = T_NOFF - ci
            kb.dma(ACT if ci % 2 == 0 else SP, T01[ci:ci + 1, :, :], T["tblD"][:, o2:o2 + 256])
        ksT = [kb.sb("ksT%d" % g, [128, 4096], BF16) for g in range(2)]
        kwT = [kb.sb("kwT%d" % g, [128, 4096], BF16) for g in range(2)]
        vs = [kb.sb("vs%d" % g, [128, 32, 128], BF16) for g in range(2)]
        vw = [kb.sb("vw%d" % g, [128, 32, 128], BF16) for g in range(2)]
        kcmpT = [kb.sb("kcmpT%d" % g, [128, 256], BF16) for g in range(2)]
        vc = [kb.sb("vc%d" % g, [128, 2, 128], BF16) for g in range(2)]
        for g in range(2):
            kb.dma(SP, ksT[g][:], T["ksT"][g])
            kb.dma(ACT, kwT[g][:], T["kwT"][g])
            kb.dma(SP, vs[g][:], T["vs"][:, g * 128:(g + 1) * 128].rearrange("(t p) f -> p t f", p=128))
            kb.dma(ACT, vw[g][:], T["vw"][:, g * 128:(g + 1) * 128].rearrange("(t p) f -> p t f", p=128))
        xc = kb.sb("xc", [128, 4096], BF16)
        w1 = kb.sb("w1", [128, 32, 128], BF16)
        w2 = kb.sb("w2", [128, 128], BF16)
        posf = kb.sb("posf", [128, 32])
        posb = kb.sb("posb", [128, 32], BF16)
        biasv = kb.sb("biasv", [128, 1])
        hs = kb.sb("hs", [128, 256], BF16)
        vccT = kb.sb("vccT", [128, 256], BF16)
        for kind in ("k", "v"):
            kb.dma(POOL, w1[:], T["w1" + kind].rearrange("(j d) h -> d j h", d=128))
            kb.dma(POOL, w2[:], T["w2" + kind])
            kb.dma(SP, posf[:], T["p_pos" + kind])
            kb.cp(DVE, posb[:], posf[:])
            for j in range(32):
                kb.mm(B[0][:, 0:1], w1[:, j, :], posb[:, j:j + 1], start=(j == 0), stop=(j == 31))
            kb.cp(DVE, biasv[:], B[0][:, 0:1])
            for g in range(2):
                kb.dma(SP, xc[:], T["kcT" if kind == "k" else "vcT"][g])
                for j in range(32):
                    kb.mm(B[1][:, 0:255], w1[:, j, :], xc[:, j:j + 16 * 254 + 1:16], start=(j == 0), stop=(j == 31))
                kb.act(hs[:, 0:255], B[1][:, 0:255], AF.Silu, bias=biasv[:, 0:1])
                kb.mm(B[2][:, 0:255], w2[:], hs[:, 0:255])
                if kind == "k":
                    kb.memset(POOL, kcmpT[g][:, 255:256], 0.0)
                    kb.cp(ACT, kcmpT[g][:, 0:255], B[2][:, 0:255])
                else:
                    kb.memset(POOL, vccT[:, 255:256], 0.0)
                    kb.cp(ACT, vccT[:, 0:255], B[2][:, 0:255])
                    for cc in range(2):
                        kb.tr(PTB[:, cc * 128:(cc + 1) * 128], vccT[:, cc * 128:(cc + 1) * 128], identb[:])
                    kb.cp(DVE, vc[g][:], PTB[:, 0:256].m(lambda a: a.rearrange("p (c f) -> p c f", c=2)))
        NB = 2
        qts = [kb.sb("qt%d" % i, [128, 8, 128], BF16) for i in range(NB)]
        gts = [kb.sb("gts%d" % i, [24, 128]) for i in range(NB)]
        xts = [kb.sb("xt%d" % i, [128, 1024]) for i in range(NB)]
        mots = [kb.sb("mot%d" % i, [128, 4, 128], BF16) for i in range(NB)]
        pTc = [kb.sb("pTc%d" % i, [128, 4, 128], BF16) for i in range(2)]
        pTs = [kb.sb("pTs%d" % i, [128, 512], BF16) for i in range(2)]
        pTw = [kb.sb("pTw%d" % i, [128, 512], BF16) for i in range(2)]
        dmx = kb.sb("dmx", [128, 4])
        impn = kb.sb("impn", [128, 64])
        t1 = kb.sb("t1", [128, 64])
        t2 = kb.sb("t2", [128, 64])
        sc = kb.sb("sc", [128, 64])
        sc2 = kb.sb("sc2", [128, 64])
        m8 = kb.sb("m8", [128, 16])
        selb = kb.sb("selb", [128, 64])
        selbT = kb.sb("selbT", [64, 128], BF16)
        rd = kb.sb("rd", [128, 128])
        tmpo = kb.sb("tmpo", [128, 128])
        mixf = kb.sb("mixf", [128, 8, 128])
        mixT = kb.sb("mixT", [128, 8, 128], BF16)
        xo = kb.sb("xo", [128, 1024])

        def combine(h, b, O, Dn, first):
            kb.ts(DVE, rd[:], Dn, 1e-30, None, ALU.max)
            kb.recip(rd[:], rd[:])
            kb.tt(DVE, rd[:], rd[:], B[6][:, b * 128:(b + 1) * 128], ALU.mult)
            if first:
                kb.tt(DVE, mixf[:, h, :], O, rd[:], ALU.mult)
            else:
                kb.tt(DVE, tmpo[:], O, rd[:], ALU.mult)
                kb.tt(POOL, mixf[:, h, :], mixf[:, h, :], tmpo[:], ALU.add)

        for qt in range(32):
            b_ = qt % NB
            cs = slice(qt * 128, (qt + 1) * 128)
            kb.dma(SP, qts[b_][:], T["q1T"][:, :, cs].rearrange("h d t -> d h t"))
            kb.dma(ACT, gts[b_][:], T["gT"][:, cs])
            kb.dma(SP, xts[b_][:], T["xb"][cs, :])
            kb.dma(ACT, mots[b_][:], T["moT"][:, :, cs].rearrange("h p t -> p h t"))
            qtile = qts[b_]
            for g in range(2):
                ncc = 2 if qt >= 16 else 1
                for cc in range(ncc):
                    m0 = 128 * qt - 2048 * cc - 31
                    for r in range(4):
                        h = 4 * g + r
                        o_ = B[cc][:, r * 128:(r + 1) * 128]
                        nb = m0 < 2145
                        kb.mm(o_, kcmpT[g][:, cc * 128:(cc + 1) * 128], qtile[:, h, :], start=True, stop=not nb)
                        if nb:
                            kb.mm(o_, identb[:], U[:, h, m0 - U_MLO: m0 - U_MLO + 128], start=False, stop=True)
                    kb.act(pTc[cc][:].m(lambda a: a.rearrange("p r q -> p (r q)")), B[cc][:], AF.Exp)
                for r in range(4):
                    for cc in range(ncc):
                        kb.mm(B[2][:, r * 128:(r + 1) * 128], vc[g][:, cc, :], pTc[cc][:, r, :], start=(cc == 0), stop=(cc == ncc - 1))
                for r in range(4):
                    for cc in range(ncc):
                        kb.mm(B[3][:, r * 128:(r + 1) * 128], onesb[:], pTc[cc][:, r, :], start=(cc == 0), stop=(cc == ncc - 1))
                for r in range(4):
                    for cc in range(ncc):
                        kb.mm(B[4][:, r * 65:(r + 1) * 65], pTc[cc][:, r, :], selmap[:, cc, :], start=(cc == 0), stop=(cc == ncc - 1))
                imp3 = B[4][:, 0:260].m(lambda a: a.rearrange("p (r s) -> p r s", r=4))
                kb.ts(DVE, dmx[:], imp3[:, :, 64], 1e-30, None, ALU.max)
                kb.recip(dmx[:], dmx[:])
                kb.ts(DVE, impn[:], imp3[:, 0, 0:64], dmx[:, 0:1], None, ALU.mult)
                for r in range(1, 4):
                    kb.stt(DVE, impn[:], imp3[:, r, 0:64], dmx[:, r:r + 1], impn[:], ALU.mult, ALU.add)
                kb.ts(DVE, t1[:], d0[:], float(2 * qt - 1), None, ALU.is_ge)
                kb.tt(DVE, t1[:], t1[:], jb100[:], ALU.mult)
                kb.tt(DVE, sc[:], impn[:], t1[:], ALU.max)
                kb.ts(DVE, t2[:], d0[:], float(2 * qt), -1000.0, ALU.is_gt, ALU.mult)
                kb.tt(DVE, sc[:], sc[:], t2[:], ALU.add)
                kb.memset(DVE, sc[:, 0:1], 100.0)
                kb.max8(m8[:, 0:8], sc[:])
                kb.match_replace(sc2[:], m8[:, 0:8], sc[:], -3000.0)
                kb.max8(m8[:, 8:16], sc2[:])
                kb.ts(DVE, t1[:], sc[:], m8[:, 15:16], None, ALU.is_ge)
                kb.ts(DVE, t2[:], sc[:], -500.0, None, ALU.is_gt)
                kb.tt(DVE, t1[:], t1[:], t2[:], ALU.mult)
                kb.ts(DVE, selb[:], t1[:], 1.0, -NEG, ALU.subtract, ALU.mult)
                kb.tr(B[5][0:64, 0:128], selb[:], ident[:])
                kb.cp(DVE, selbT[:], B[5][0:64, 0:128])
                for r in range(4):
                    h = 4 * g + r
                    for b in range(3):
                        kb.mm(B[6][:, b * 128:(b + 1) * 128], selrow[:, h * 3 + b, :], gts[b_][:])
                    combine(h, 0, B[2][:, r * 128:(r + 1) * 128], B[3][:, r * 128:(r + 1) * 128], True)
                for r in range(4):
                    h = 4 * g + r
                    for b in range(3):
                        kb.mm(B[6][:, b * 128:(b + 1) * 128], selrow[:, h * 3 + b, :], gts[b_][:])
                    nkt = qt + 1
                    OS, DS = B[2][:, 0:128], B[3][:, 0:128]
                    for kg in range((nkt + 3) // 4):
                        kts = list(range(kg * 4, min(nkt, kg * 4 + 4)))
                        sb_ = B[kg % 2]
                        for i_, kt in enumerate(kts):
                            o_ = sb_[:, i_ * 128:(i_ + 1) * 128]
                            near = kt >= qt - 1
                            kb.mm(o_, ksT[g][:, kt * 128:(kt + 1) * 128], qtile[:, h, :], start=True, stop=False)
                            kb.mm(o_, E[:, kt * 128:(kt + 1) * 128], selbT[:], start=False, stop=not near)
                            if kt == qt:
                                kb.mm(o_, identb[:], T01[:, h, 0:128], start=False, stop=True)
                            elif kt == qt - 1:
                                kb.mm(o_, identb[:], T01[:, h, 128:256], start=False, stop=True)
                        nn = len(kts) * 128
                        pt_ = pTs[kg % 2]
                        kb.act(pt_[:, 0:nn], sb_[:, 0:nn], AF.Exp)
                        for i_, kt in enumerate(kts):
                            kb.mm(OS, vs[g][:, kt, :], pt_[:, i_ * 128:(i_ + 1) * 128], start=(kt == 0), stop=(kt == qt))
                        for i_, kt in enumerate(kts):
                            kb.mm(DS, onesb[:], pt_[:, i_ * 128:(i_ + 1) * 128], start=(kt == 0), stop=(kt == qt))
                    combine(h, 1, OS, DS, False)
                    offs = [o for o in range(4, -1, -1) if qt - o >= 0]
                    OW, DW = B[4][:, 0:128], B[5][:, 0:128]
                    slots = []
                    for o in offs:
                        kt = qt - o
                        if o >= 1:
                            bank, col = B[0], (o - 1) * 128
                        else:
                            bank, col = B[1], 0
                        o_ = bank[:, col:col + 128]
                        hasb = o in (0, 1, 4)
                        kb.mm(o_, kwT[g][:, kt * 128:(kt + 1) * 128], qtile[:, h, :], start=True, stop=not hasb)
                        if o == 0:
                            kb.mm(o_, identb[:], T01[:, h, 0:128], start=False, stop=True)
                        elif o == 1:
                            kb.mm(o_, identb[:], T01[:, h, 128:256], start=False, stop=True)
                        elif o == 4:
                            kb.mm(o_, identb[:], m4t[:], start=False, stop=True)
                        slots.append((o, kt))
                    far = [o for o in offs if o >= 1]
                    if far:
                        lo, hi = (min(far) - 1) * 128, max(far) * 128
                        kb.act(pTw[0][:, lo:hi], B[0][:, lo:hi], AF.Exp)
                    kb.act(pTw[1][:, 0:128], B[1][:, 0:128], AF.Exp)
                    pw = lambda o: pTw[0][:, (o - 1) * 128: o * 128] if o >= 1 else pTw[1][:, 0:128]
                    for i_, (o, kt) in enumerate(slots):
                        kb.mm(OW, vw[g][:, kt, :], pw(o), start=(i_ == 0), stop=(i_ == len(slots) - 1))
                    for i_, (o, kt) in enumerate(slots):
                        kb.mm(DW, onesb[:], pw(o), start=(i_ == 0), stop=(i_ == len(slots) - 1))
                    combine(h, 2, OW, DW, False)
            kb.cp(ACT, mixT[:], mixf[:])
            if "mixD" in T:
                kb.dma(SP, T["mixD"][qt], mixf[:])
            for n in range(2):
                o_ = B[n][:]
                for fc in range(8):
                    kb.mm(o_, mixT[:, fc, :], wout[:, fc, n * 512:(n + 1) * 512], start=(fc == 0), stop=False)
                for h4 in range(4):
                    kb.mm(o_, mots[b_][:, h4, :], wout[:, 8 + h4, n * 512:(n + 1) * 512], start=False, stop=(h4 == 3))
                kb.tt(DVE, xo[:, n * 512:(n + 1) * 512], xts[b_][:, n * 512:(n + 1) * 512], o_, ALU.add)
            kb.dma(SP, T["xc"][cs, :], xo[:])


SCRATCH = {
    "qT": ((8, 128, S_LEN), F32), "kT": ((8, 128, S_LEN), F32), "kTM": ((S_LEN, 1024), F32),
    "vTM": ((S_LEN, 1024), F32), "zs": ((S_LEN, 1024), F32), "bg": ((S_LEN, 16), F32),
    "moT": ((4, 128, S_LEN), BF16), "xa": ((S_LEN, DM), F32), "xb": ((S_LEN, DM), F32), "xc": ((S_LEN, DM), F32),
    "q1T": ((8, 128, S_LEN), BF16), "kcT": ((2, 128, S_LEN), BF16), "vcT": ((2, 128, S_LEN), BF16),
    "ksT": ((2, 128, S_LEN), BF16), "kwT": ((2, 128, S_LEN), BF16), "vs": ((S_LEN, 256), BF16),
    "vw": ((S_LEN, 256), BF16), "gT": ((24, S_LEN), F32), "tblD": ((8, T_L), BF16),
}
NP_DT = {F32: np.float32, BF16: bf}


def build(small_shapes, dbg=(), phases="abcdef"):
    nc = bass.Bass("TRN2", target_bir_lowering=False)
    T = {}
    for name, shp in BIG_INPUTS.items():
        T[name] = nc.dram_tensor(name, list(shp), F32, kind="ExternalInput").ap()
    for name, (shp, dt) in small_shapes.items():
        T[name] = nc.dram_tensor(name, list(shp), dt, kind="ExternalInput").ap()
    for name, (shp, dt) in SCRATCH.items():
        kind = "ExternalOutput" if name in dbg else "Internal"
        T[name] = nc.dram_tensor(name, list(shp), dt, kind=kind).ap()
    T["y"] = nc.dram_tensor("y", [S_LEN, DM], F32, kind="ExternalOutput").ap()
    if "mixD" in dbg:
        T["mixD"] = nc.dram_tensor("mixD", [32, 128, 8, 128], F32, kind="ExternalOutput").ap()
    kb = KB(nc)
    if "a" in phases:
        phase_l0_in(kb, T)
    if "b" in phases:
        phase_l0_gdn(kb, T)
    if "c" in phases:
        phase_ffn(kb, T, 0, T["xa"], T["xb"], False)
    if "d" in phases:
        phase_l1_in(kb, T)
    if "e" in phases:
        phase_l1_attn(kb, T)
    if "f" in phases:
        phase_ffn(kb, T, 1, T["xc"], T["y"], True)
    return nc, kb


def prep_inputs(inp):
    shared = {}
    shared["mem_w_kv"] = np.ascontiguousarray(inp["mem_w_kv"], np.float32)
    shared["w_out"] = np.ascontiguousarray(inp["w_out"], np.float32)
    shared["gdn_w_in"] = np.ascontiguousarray(inp["gdn_w_in"][0], np.float32)
    shared["nsa_w_in"] = np.ascontiguousarray(inp["nsa_w_in"][0], np.float32)
    shared["ffn_w_up"] = np.ascontiguousarray(inp["ffn_w_up"], np.float32)
    shared["ffn_w_down"] = np.ascontiguousarray(inp["ffn_w_down"], np.float32)
    shared["w1k"] = np.ascontiguousarray(inp["nsa_cmp_w1_k"][0], np.float32)
    shared["w2k"] = np.ascontiguousarray(inp["nsa_cmp_w2_k"][0], np.float32)
    shared["w1v"] = np.ascontiguousarray(inp["nsa_cmp_w1_v"][0], np.float32)
    shared["w2v"] = np.ascontiguousarray(inp["nsa_cmp_w2_v"][0], np.float32)
    small = {}
    small.update(host_consts())
    small.update(host_layout(inp))
    small_shapes = {k: (v.shape, BF16 if v.dtype == bf else F32) for k, v in small.items()}
    shared.update(small)
    return shared, small_shapes


def kernel(**inp):
    inp = {k: np.asarray(v) for k, v in inp.items()}
    shared, small_shapes = prep_inputs(inp)
    nc, kb = build(small_shapes)
    n = inp["x"].shape[0]
    in_maps = []
    for b in range(n):
        m = dict(shared)
        m["x"] = np.ascontiguousarray(inp["x"][b], np.float32)
        m["mem"] = np.ascontiguousarray(inp["mem"][b], np.float32)
        in_maps.append(m)
    res = run_bass_kernel_spmd(nc, in_maps, core_ids=list(range(n)))
    return np.stack([np.asarray(r["y"], np.float32) for r in res.results], axis=0)
```

```python
import contextlib
import numpy as np
import ml_dtypes
import concourse.bass as bass
import concourse.mybir as mybir
from concourse.bass_utils import run_bass_kernel_spmd

F32 = mybir.dt.float32
BF16 = mybir.dt.bfloat16
ALU = mybir.AluOpType
AF = mybir.ActivationFunctionType
AX = mybir.AxisListType
PE, ACT, DVE, POOL, SP = "pe", "act", "dve", "pool", "sp"
ENGS = (PE, ACT, DVE, POOL, SP)


class Res:
    __slots__ = ("name", "last_w", "readers")

    def __init__(self, name):
        self.name = name
        self.last_w = None
        self.readers = []


class Op:
    __slots__ = ("eng", "fn", "reads", "writes", "is_dma", "deps", "needs_inc", "done", "extra")


class V:
    __slots__ = ("ap", "res")

    def __init__(self, ap, res):
        self.ap = ap
        self.res = res

    def __getitem__(self, key):
        return V(self.ap[key], self.res)

    def m(self, f):
        return V(f(self.ap), self.res)

    def bc(self, shape):
        return V(self.ap.to_broadcast(list(shape)), self.res)


class TT:
    def __init__(self, h, name):
        self.h = h
        self.res = Res(name)

    def __getitem__(self, key):
        return V(self.h[key], self.res)


class Sched:
    W = 20000
    M = 6

    def __init__(self, nc):
        self.nc = nc
        self.ops = []
        self.engs = {PE: nc.tensor, ACT: nc.scalar, DVE: nc.vector, POOL: nc.gpsimd, SP: nc.sync}
        self.sem_pool = {}
        self.cnt = {k: 0 for k in ENGS}
        self.dcnt = {k: 0 for k in ENGS}
        self.waited = {k: {} for k in ENGS}
        self.dq = {k: [] for k in ENGS}
        self.last_compute = {k: None for k in ENGS}
        self.recent_dma = []
        self.n_ops = 0
        self.n_wait = 0

    def op(self, eng, fn, reads=(), writes=(), dma=False, extra=()):
        o = Op()
        o.eng = eng
        o.fn = fn
        o.reads = [r for r in reads if r is not None]
        o.writes = [w for w in writes if w is not None]
        o.is_dma = dma
        o.deps = []
        o.needs_inc = False
        o.done = None
        o.extra = list(extra)
        self.ops.append(o)
        if dma:
            self.recent_dma.append(o)
        elif fn is not None:
            self.last_compute[eng] = o
        return o

    def barrier(self):
        deps = [o for o in self.last_compute.values() if o is not None] + list(self.recent_dma)
        for e in ENGS:
            self.op(e, None, extra=deps)
        self.recent_dma = []

    def _sem(self, key):
        if key not in self.sem_pool:
            self.sem_pool[key] = self.nc.alloc_semaphore("s_%s_%s" % key)
        return self.sem_pool[key]

    def flush(self):
        ops = self.ops
        self.ops = []
        for o in ops:
            deps = set(o.extra)
            for r in o.reads:
                if r.last_w is not None:
                    deps.add(r.last_w)
            for w in o.writes:
                if w.last_w is not None:
                    deps.add(w.last_w)
                for rd in w.readers:
                    deps.add(rd)
            for r in o.reads:
                if r not in o.writes:
                    r.readers.append(o)
            for w in o.writes:
                w.last_w = o
                w.readers = []
            final = []
            for d in deps:
                if d is o:
                    continue
                if (not d.is_dma) and (not o.is_dma) and d.eng == o.eng and o.eng == PE and o.fn is not None:
                    continue
                final.append(d)
            o.deps = final
            for d in final:
                if not d.is_dma:
                    assert d.done is None or d.needs_inc, "dependency on already-emitted op without inc"
                    d.needs_inc = True
        for o in ops:
            if o.fn is None:
                continue
            if o.is_dma:
                n = self.dcnt[o.eng]
                self.dcnt[o.eng] += 1
                o.done = (("d" + o.eng, n % self.M), 16 * (n // self.M + 1))
            elif o.needs_inc:
                n = self.cnt[o.eng]
                self.cnt[o.eng] += 1
                o.done = ((o.eng, n // self.W), n % self.W + 1)
        for o in ops:
            e = self.engs[o.eng]
            wt = self.waited[o.eng]
            need = {}
            for d in o.deps:
                if d.done is None:
                    continue
                key, val = d.done
                if need.get(key, 0) < val:
                    need[key] = val
            if o.is_dma:
                q = self.dq[o.eng]
                if len(q) >= self.M:
                    key, val = q[len(q) - self.M].done
                    if need.get(key, 0) < val:
                        need[key] = val
                q.append(o)
            for key, val in need.items():
                if wt.get(key, 0) >= val:
                    continue
                wt[key] = val
                e.wait_ge(self._sem(key), val)
                self.n_wait += 1
            if o.fn is None:
                continue
            ins = o.fn(e)
            self.n_ops += 1
            if o.is_dma:
                ins.then_inc(self._sem(o.done[0]), 16)
            elif o.needs_inc:
                ins.then_inc(self._sem(o.done[0]), 1)


class KB:
    def __init__(self, nc):
        self.nc = nc
        self.S = Sched(nc)
        self.stack = None
        self.uid = 0

    @contextlib.contextmanager
    def phase(self, name):
        with contextlib.ExitStack() as st:
            prev = self.stack
            self.stack = st
            self.pname = name
            yield
            self.S.barrier()
            self.S.flush()
            self.stack = prev

    def sb(self, name, shape, dt=F32):
        self.uid += 1
        h = self.stack.enter_context(self.nc.sbuf_tensor("%s_%s_%d" % (self.pname, name, self.uid), list(shape), dt))
        return TT(h, name)

    def ps(self, name, shape=(128, 512), dt=F32):
        self.uid += 1
        h = self.stack.enter_context(self.nc.psum_tensor("%s_%s_%d" % (self.pname, name, self.uid), list(shape), dt))
        return TT(h, name)

    def dma(self, eng, out, in_, **kw):
        rd = [in_.res] if isinstance(in_, V) else []
        wr = [out.res] if isinstance(out, V) else []
        oa = out.ap if isinstance(out, V) else out
        ia = in_.ap if isinstance(in_, V) else in_
        self.S.op(eng, lambda e: e.dma_start(out=oa, in_=ia, **kw), rd, wr, dma=True)

    def mm(self, out, lhsT, rhs, start=True, stop=True, **kw):
        self.S.op(PE, lambda e: e.matmul(out=out.ap, lhsT=lhsT.ap, rhs=rhs.ap, start=start, stop=stop, **kw),
                  [lhsT.res, rhs.res], [out.res])

    def tr(self, out, in_, ident):
        self.S.op(PE, lambda e: e.transpose(out=out.ap, in_=in_.ap, identity=ident.ap),
                  [in_.res, ident.res], [out.res])

    def act(self, out, in_, func, bias=None, scale=1.0, accum=None, eng=ACT):
        rd = [in_.res]
        kw = {}
        if bias is not None:
            if isinstance(bias, V):
                rd.append(bias.res)
                kw["bias"] = bias.ap
            else:
                kw["bias"] = bias
        if isinstance(scale, V):
            rd.append(scale.res)
            kw["scale"] = scale.ap
        else:
            kw["scale"] = scale
        wr = [out.res]
        if accum is not None:
            wr.append(accum.res)
            kw["accum_out"] = accum.ap
        self.S.op(ACT, lambda e: e.activation(out=out.ap, in_=in_.ap, func=func, **kw), rd, wr)

    def _sc(self, s, rd):
        if isinstance(s, V):
            rd.append(s.res)
            return s.ap
        return s

    def ts(self, eng, out, in0, s1, s2, op0, op1=None, accum=None):
        rd = [in0.res]
        a1 = self._sc(s1, rd)
        a2 = self._sc(s2, rd)
        wr = [out.res]
        kw = {}
        if op1 is not None:
            kw["op1"] = op1
        if accum is not None:
            wr.append(accum.res)
            kw["accum_out"] = accum.ap
        self.S.op(eng, lambda e: e.tensor_scalar(out=out.ap, in0=in0.ap, scalar1=a1, scalar2=a2, op0=op0, **kw), rd, wr)

    def tt(self, eng, out, in0, in1, op):
        self.S.op(eng, lambda e: e.tensor_tensor(out=out.ap, in0=in0.ap, in1=in1.ap, op=op),
                  [in0.res, in1.res], [out.res])

    def stt(self, eng, out, in0, s, in1, op0, op1):
        rd = [in0.res, in1.res]
        a = self._sc(s, rd)
        self.S.op(eng, lambda e: e.scalar_tensor_tensor(out=out.ap, in0=in0.ap, scalar=a, in1=in1.ap, op0=op0, op1=op1),
                  rd, [out.res])

    def cp(self, eng, out, in_):
        if eng == ACT:
            self.S.op(ACT, lambda e: e.copy(out=out.ap, in_=in_.ap), [in_.res], [out.res])
        else:
            self.S.op(eng, lambda e: e.tensor_copy(out=out.ap, in_=in_.ap), [in_.res], [out.res])

    def memset(self, eng, out, val):
        self.S.op(eng, lambda e: e.memset(out.ap, val), [], [out.res])

    def recip(self, out, in_):
        self.S.op(DVE, lambda e: e.reciprocal(out=out.ap, in_=in_.ap), [in_.res], [out.res])

    def reduce(self, eng, out, in_, op, axis=AX.X):
        self.S.op(eng, lambda e: e.tensor_reduce(out=out.ap, in_=in_.ap, axis=axis, op=op), [in_.res], [out.res])

    def max8(self, out, in_):
        self.S.op(DVE, lambda e: e.max(out=out.ap, in_=in_.ap), [in_.res], [out.res])

    def match_replace(self, out, rep, vals, imm):
        self.S.op(DVE, lambda e: e.match_replace(out=out.ap, in_to_replace=rep.ap, in_values=vals.ap, imm_value=imm),
                  [rep.res, vals.res], [out.res])

S_LEN = 4096
DM = 1024
EPS = 1e-6
NEG = -30000.0
U_MLO = -127
U_W = 2400
T_NOFF = 2176
T_L = 4480
bf = ml_dtypes.bfloat16


def _bucket_np(n):
    n = np.maximum(n, 0)
    logv = np.log(np.maximum(n, 1).astype(np.float32) / np.float32(16)) / np.float32(np.log(128 / 16))
    large = 16 + (logv.astype(np.float32) * np.float32(16)).astype(np.int32)
    large = np.minimum(large, 31)
    return np.where(n < 16, n, large)


def host_consts():
    C = {}
    i = np.arange(128)
    same = (i[:, None] // 64) == (i[None, :] // 64)
    C["c_ident"] = np.eye(128, dtype=np.float32)
    C["c_tri"] = (same & (i[:, None] <= i[None, :])).astype(np.float32)
    C["c_sellast"] = (i[:, None] == (64 * (i[None, :] // 64) + 63)).astype(np.float32)
    C["c_sela"] = np.repeat((i == 63).astype(np.float32)[:, None], 128, 1)
    C["c_selb"] = np.repeat((i == 127).astype(np.float32)[:, None], 128, 1)
    C["c_maskneg"] = np.where(same & (i[:, None] >= i[None, :]), 0.0, -NEG).astype(np.float32)
    C["c_strict"] = (same & (i[:, None] > i[None, :])).astype(np.float32)
    n = np.arange(T_L) - T_NOFF
    oh = np.zeros((33, T_L), np.float32)
    b = _bucket_np(n)
    for t in range(T_L):
        if n[t] >= 0:
            oh[b[t], t] += 1.0
            oh[31, t] -= 1.0
        else:
            oh[32, t] = NEG
    C["c_ohx"] = oh
    C["c_m4t"] = np.where(i[None, :] < i[:, None], 0.0, NEG).astype(bf)
    C["c_e"] = (np.arange(64)[:, None] == (np.arange(4096)[None, :] // 64)).astype(bf)
    c_start = np.arange(255) * 16
    c_end = c_start + 31
    s_start = np.arange(64) * 64
    sm = ((c_start[:, None] < s_start[None, :] + 64) & (s_start[None, :] <= c_end[:, None])).astype(np.float32)
    smx = np.zeros((256, 65), np.float32)
    smx[:255, :64] = sm
    smx[:, 64] = 1.0
    C["c_selmap"] = smx.reshape(2, 128, 65).transpose(1, 0, 2).astype(bf).copy()
    C["c_d0"] = (np.arange(64)[None, :] - (i[:, None] >= 64)).astype(np.float32)
    C["c_jb100"] = np.repeat((100.0 + np.arange(64, dtype=np.float32))[None, :], 128, 0)
    sr = np.zeros((24, 24, 128), np.float32)
    for hb in range(24):
        sr[hb, hb, :] = 1.0
    C["c_selrow"] = sr
    return C


def host_layout(inp):
    L = {}
    col = lambda v: np.ascontiguousarray(v.reshape(-1, 128).T).astype(np.float32)
    rep = lambda v: np.ascontiguousarray(np.broadcast_to(v[None, :], (128, v.shape[0]))).astype(np.float32)
    for i in range(2):
        L["p_gmix%d" % i] = col(inp["norm_mix_w"][i])
        L["p_gffn%d" % i] = col(inp["norm_ffn_w"][i])
        L["p_gmem%d" % i] = col(inp["mem_norm_w"][i])
        L["p_fcw%d" % i] = np.ascontiguousarray(inp["ffn_conv_w"][i].reshape(3, 22, 128).transpose(2, 0, 1))
        L["p_fcb%d" % i] = col(inp["ffn_conv_b"][i])
    L["p_fw"] = rep(inp["final_norm_w"])
    L["p_gcw"] = np.ascontiguousarray(inp["gdn_conv_w"][0].reshape(4, 24, 128).transpose(2, 1, 0))
    L["p_alog"] = rep(inp["gdn_a_log"][0])
    L["p_dtb"] = rep(inp["gdn_dt_bias"][0])
    L["p_gnw"] = rep(inp["gdn_norm_w"][0])
    L["p_posk"] = np.ascontiguousarray(inp["nsa_cmp_pos_k"][0].T)
    L["p_posv"] = np.ascontiguousarray(inp["nsa_cmp_pos_v"][0].T)
    rbx = np.ones((33, 8), np.float32)
    rbx[:32] = inp["rel_bias"]
    L["p_rbx"] = rbx
    return L


BIG_INPUTS = {
    "x": (S_LEN, DM), "mem": (256, DM), "mem_w_kv": (2, 1024, 1024), "w_out": (2, 1536, 1024),
    "gdn_w_in": (1024, 4624), "nsa_w_in": (1024, 3096), "ffn_w_up": (2, 1024, 5632),
    "ffn_w_down": (2, 2816, 1024), "w1k": (4096, 128), "w2k": (128, 128), "w1v": (4096, 128), "w2v": (128, 128),
}


def rms_to_hT(kb, W, xt, gcol, hT, c0):
    kb.memset(DVE, W["ss"][:], 0.0)
    kb.act(W["junk"][:], xt, AF.Square, accum=W["ss"][:])
    kb.act(W["rstd"][:], W["ss"][:], AF.Sqrt, scale=1.0 / DM, bias=W["epsc"][:, 0:1])
    kb.recip(W["rstd"][:], W["rstd"][:])
    kb.ts(DVE, W["xn"][:], xt, W["rstd"][:, 0:1], None, ALU.mult)
    for k in range(8):
        kb.tr(W["ptr"][:, k * 128:(k + 1) * 128], W["xn"][:, k * 128:(k + 1) * 128], W["identb"][:])
    kb.tt(DVE, hT[:, :, c0:c0 + 128], W["ptr"][:].m(lambda a: a.rearrange("p (k t) -> p k t", k=8)),
          gcol[:].m(lambda a: a.unsqueeze(2).to_broadcast([128, 8, 128])), ALU.mult)


def common_work(kb, T):
    W = {}
    W["ident"] = kb.sb("ident", [128, 128])
    kb.dma(SP, W["ident"][:], T["c_ident"])
    W["identb"] = kb.sb("identb", [128, 128], BF16)
    kb.cp(DVE, W["identb"][:], W["ident"][:])
    W["ones"] = kb.sb("ones", [128, 128])
    kb.memset(POOL, W["ones"][:], 1.0)
    W["onesb"] = kb.sb("onesb", [128, 128], BF16)
    kb.memset(POOL, W["onesb"][:], 1.0)
    W["epsc"] = kb.sb("epsc", [128, 1])
    kb.memset(POOL, W["epsc"][:], EPS)
    W["ss"] = kb.sb("ss", [128, 1])
    W["rstd"] = kb.sb("rstd", [128, 1])
    W["junk"] = kb.sb("junk", [128, 1024])
    W["xn"] = kb.sb("xn", [128, 1024], BF16)
    W["ptr"] = kb.ps("ptr", [128, 1024], BF16)
    return W


def load_w_bf16(kb, name, dram2d, nk, ncol):
    w = kb.sb(name, [128, nk, ncol], BF16)
    for k in range(nk):
        kb.dma(POOL, w[:, k, :], dram2d[k * 128:(k + 1) * 128, :])
    return w


def mem_prep(kb, T, W, i, PA, PB):
    wkv = load_w_bf16(kb, "wkv", T["mem_w_kv"][i], 8, 1024)
    gcm = kb.sb("gcm", [128, 8])
    kb.dma(SP, gcm[:], T["p_gmem%d" % i])
    memnT = kb.sb("memnT", [128, 8, 256], BF16)
    for a in range(2):
        xt = kb.sb("memx%d" % a, [128, 1024])
        kb.dma(SP, xt[:], T["mem"][a * 128:(a + 1) * 128, :])
        rms_to_hT(kb, W, xt[:], gcm, memnT, a * 128)
    kmemT = kb.sb("kmemT", [128, 4, 256], BF16)
    vmem = kb.sb("vmem", [128, 2, 512], BF16)
    for h in range(4):
        for k in range(8):
            kb.mm(PA[:, 0:256], wkv[:, k, h * 128:(h + 1) * 128], memnT[:, k, :], start=(k == 0), stop=(k == 7))
        kb.cp(ACT, kmemT[:, h, :], PA[:, 0:256])
    for mc in range(2):
        for k in range(8):
            kb.mm(PB[:], memnT[:, k, mc * 128:(mc + 1) * 128], wkv[:, k, 512:1024], start=(k == 0), stop=(k == 7))
        kb.cp(DVE, vmem[:, mc, :], PB[:])
    return kmemT, vmem


def mem_attn(kb, W, qmT, kmemT, vmem, moTst, PS0, PS1, PO, PD, pTs, rden):
    for h in range(4):
        for mc, ps in ((0, PS0), (1, PS1)):
            kb.mm(ps[:], kmemT[:, h, mc * 128:(mc + 1) * 128], qmT[:, h, :])
            kb.act(pTs[mc][:], ps[:], AF.Exp)
        for mc in range(2):
            kb.mm(PO[:], vmem[:, mc, h * 128:(h + 1) * 128], pTs[mc][:], start=(mc == 0), stop=(mc == 1))
        for mc in range(2):
            kb.mm(PD[:], W["onesb"][:], pTs[mc][:], start=(mc == 0), stop=(mc == 1))
        kb.recip(rden[:], PD[:])
        kb.tt(DVE, moTst[:, h, :], PO[:], rden[:], ALU.mult)


def phase_l0_in(kb, T):
    with kb.phase("l0in"):
        W = common_work(kb, T)
        PA = [kb.ps("PA%d" % i) for i in range(4)]
        PBs = [kb.ps("PB%d" % i) for i in range(2)]
        PB = PBs[0]
        PT = kb.ps("PT")
        PS0, PS1, PO = PA[2], PA[3], PBs[1]
        win = load_w_bf16(kb, "win", T["gdn_w_in"], 8, 4624)
        gcol = kb.sb("gcol", [128, 8])
        kb.dma(SP, gcol[:], T["p_gmix0"])
        cwT = kb.sb("cwT", [128, 24, 4])
        kb.dma(SP, cwT[:], T["p_gcw"])
        alog = kb.sb("alog", [128, 8])
        kb.dma(SP, alog[:], T["p_alog"])
        dtb = kb.sb("dtb", [128, 8])
        kb.dma(SP, dtb[:], T["p_dtb"])
        negA = kb.sb("negA", [128, 8])
        kb.act(negA[:], alog[:], AF.Exp)
        kb.ts(DVE, negA[:], negA[:], -1.0, None, ALU.mult)
        kmemT, vmem = mem_prep(kb, T, W, 0, PA[0], PA[1])
        halo = kb.sb("halo", [128, 24, 3])
        kb.memset(POOL, halo[:], 0.0)
        hTs = [kb.sb("hT%d" % i, [128, 8, 512], BF16) for i in range(2)]
        xts = [kb.sb("xt%d" % i, [128, 1024]) for i in range(2)]
        NBF = 4
        pres = [kb.sb("pre%d" % i, [128, 515]) for i in range(NBF)]
        accs = [kb.sb("acc%d" % i, [128, 512]) for i in range(NBF)]
        ys = [kb.sb("y%d" % i, [128, 512]) for i in range(NBF)]
        sqs = [kb.sb("sq%d" % i, [128, 512]) for i in range(2)]
        rns = [kb.sb("rn%d" % i, [128, 512]) for i in range(2)]
        stg = [kb.sb("stg%d" % i, [128, 4, 128]) for i in range(2)]
        zts = [kb.sb("zt%d" % i, [128, 1024]) for i in range(2)]
        bgt = [kb.sb("bgt%d" % i, [128, 16]) for i in range(2)]
        xx = kb.sb("xx", [128, 8])
        qmT = kb.sb("qmT", [128, 4, 512], BF16)
        pTs = [kb.sb("pTs%d" % i, [128, 512], BF16) for i in range(2)]
        rden = kb.sb("rden", [128, 512])
        moTst = kb.sb("moTst", [128, 4, 512], BF16)
        def norm_st(st):
            for a in range(4):
                xt = xts[a % 2]
                kb.dma(SP if a % 2 == 0 else ACT, xt[:], T["x"][st * 512 + a * 128: st * 512 + (a + 1) * 128, :])
                rms_to_hT(kb, W, xt[:], gcol, hTs[st % 2], a * 128)

        norm_st(0)
        for st in range(8):
            hT = hTs[st % 2]
            cs = slice(st * 512, (st + 1) * 512)
            for c in range(24):
                pa = PA[c % 4]
                PB = PBs[c % 2]
                sq, rn = sqs[c % 2], rns[c % 2]
                for k in range(8):
                    kb.mm(pa[:], win[:, k, c * 128:(c + 1) * 128], hT[:, k, :], start=(k == 0), stop=(k == 7))
                pre, acc, y = pres[c % NBF], accs[c % NBF], ys[c % NBF]
                kb.cp(POOL, pre[:, 0:3], halo[:, c, :])
                kb.cp(ACT, pre[:, 3:515], pa[:])
                kb.cp(POOL, halo[:, c, :], pre[:, 512:515])
                kb.ts(DVE, acc[:], pre[:, 3:515], cwT[:, c, 3:4], None, ALU.mult)
                kb.stt(DVE, acc[:], pre[:, 2:514], cwT[:, c, 2:3], acc[:], ALU.mult, ALU.add)
                kb.stt(DVE, acc[:], pre[:, 1:513], cwT[:, c, 1:2], acc[:], ALU.mult, ALU.add)
                kb.stt(DVE, acc[:], pre[:, 0:512], cwT[:, c, 0:1], acc[:], ALU.mult, ALU.add)
                kb.act(y[:], acc[:], AF.Silu)
                if c < 16:
                    kb.tt(POOL, sq[:], y[:], y[:], ALU.mult)
                    kb.mm(PB[:], W["ones"][:], sq[:])
                    sc_ = 128.0 if c < 8 else 1.0
                    kb.act(rn[:], PB[:], AF.Sqrt, scale=sc_, bias=W["epsc"][:, 0:1])
                    kb.recip(rn[:], rn[:])
                    kb.tt(DVE, y[:], y[:], rn[:], ALU.mult)
                if c < 8:
                    kb.dma(SP, T["qT"][c, :, cs], y[:])
                else:
                    if c < 16:
                        kb.dma(SP, T["kT"][c - 8, :, cs], y[:])
                    for a in range(4):
                        kb.tr(PT[:, a * 128:(a + 1) * 128], y[:, a * 128:(a + 1) * 128], W["ident"][:])
                    sg = stg[c % 2]
                    kb.cp(ACT, sg[:], PT[:].m(lambda ap: ap.rearrange("p (a f) -> p a f", a=4)))
                    dst = T["kTM"] if c < 16 else T["vTM"]
                    hh = (c - 8) % 8
                    kb.dma(ACT, dst[cs, hh * 128:(hh + 1) * 128].rearrange("(a p) f -> p a f", p=128), sg[:])
                if c == 12 and st + 1 < 8:
                    norm_st(st + 1)
            for h4 in range(4):
                pa = PA[h4 % 2]
                for k in range(8):
                    kb.mm(pa[:], win[:, k, 4112 + h4 * 128: 4112 + (h4 + 1) * 128], hT[:, k, :], start=(k == 0), stop=(k == 7))
                kb.ts(DVE, qmT[:, h4, :], pa[:], 128.0 ** -0.5, None, ALU.mult)
            mem_attn(kb, W, qmT, kmemT, vmem, moTst, PS0, PS1, PO, PBs[0], pTs, rden)
            kb.dma(SP, T["moT"][:, :, cs].rearrange("h p t -> p h t"), moTst[:])
            for a in range(4):
                ts_ = slice(a * 128, (a + 1) * 128)
                rows = slice(st * 512 + a * 128, st * 512 + (a + 1) * 128)
                zt = zts[a % 2]
                for n in range(2):
                    for k in range(8):
                        kb.mm(PA[n][:], hT[:, k, ts_], win[:, k, 3072 + n * 512: 3072 + (n + 1) * 512], start=(k == 0), stop=(k == 7))
                    kb.act(zt[:, n * 512:(n + 1) * 512], PA[n][:], AF.Silu)
                kb.dma(SP, T["zs"][rows, :], zt[:])
                for k in range(8):
                    kb.mm(PT[:, 0:16], hT[:, k, ts_], win[:, k, 4096:4112], start=(k == 0), stop=(k == 7))
                bg = bgt[a % 2]
                kb.act(bg[:, 0:8], PT[:, 0:8], AF.Sigmoid)
                kb.tt(DVE, xx[:], PT[:, 8:16], dtb[:], ALU.add)
                kb.act(xx[:], xx[:], AF.Exp)
                kb.act(xx[:], xx[:], AF.Ln, bias=1.0)
                kb.tt(DVE, bg[:, 8:16], xx[:], negA[:], ALU.mult)
                kb.dma(ACT, T["bg"][rows, :], bg[:])


def out_proj(kb, wout, mixT, mot, xt, xo, P2, dst_rows):
    for n in range(2):
        o_ = P2[:, n * 512:(n + 1) * 512]
        for fc in range(8):
            kb.mm(o_, mixT[:, fc, :], wout[:, fc, n * 512:(n + 1) * 512], start=(fc == 0), stop=False)
        for h4 in range(4):
            kb.mm(o_, mot[:, h4, :], wout[:, 8 + h4, n * 512:(n + 1) * 512], start=False, stop=(h4 == 3))
    kb.tt(DVE, xo[:], xt[:], P2[:], ALU.add)
    kb.dma(SP, dst_rows, xo[:])


def phase_l0_gdn(kb, T):
    with kb.phase("l0gdn"):
        ident = kb.sb("ident", [128, 128])
        kb.dma(SP, ident[:], T["c_ident"])
        cn = {}
        for nm in ("tri", "sellast", "sela", "selb", "maskneg", "strict"):
            cn[nm] = kb.sb(nm, [128, 128])
            kb.dma(SP, cn[nm][:], T["c_" + nm])
        ones = kb.sb("ones", [128, 128])
        kb.memset(POOL, ones[:], 1.0)
        epsc = kb.sb("epsc", [128, 1])
        kb.memset(POOL, epsc[:], EPS)
        wout = load_w_bf16(kb, "wout", T["w_out"][0], 12, 1024)
        nw = kb.sb("nw", [128, 128])
        kb.dma(SP, nw[:], T["p_gnw"])
        S_ = kb.sb("S", [128, 8, 128])
        kb.memset(POOL, S_[:], 0.0)
        P = [kb.ps("P%d" % i, [128, 1024]) for i in range(4)]
        v3 = lambda t: t[:].m(lambda a: a.rearrange("p (h j) -> p h j", h=8))
        P3 = [v3(p) for p in P]
        bc8 = lambda v: v.m(lambda a: a.unsqueeze(1).to_broadcast([128, 8, 128]))
        sc8 = lambda v: v.m(lambda a: a.unsqueeze(2).to_broadcast([128, 8, 128]))
        NB = 2
        qTt = [kb.sb("qTt%d" % i, [128, 8, 128]) for i in range(NB)]
        kTt = [kb.sb("kTt%d" % i, [128, 8, 128]) for i in range(NB)]
        kTMt = [kb.sb("kTMt%d" % i, [128, 8, 128]) for i in range(NB)]
        vTMt = [kb.sb("vTMt%d" % i, [128, 8, 128]) for i in range(NB)]
        kf = [kb.sb("kf%d" % i, [64, 2, 8, 128]) for i in range(NB)]
        zf = [kb.sb("zf%d" % i, [64, 2, 1024]) for i in range(1)] * NB
        bgts = [kb.sb("bgt%d" % i, [128, 16]) for i in range(NB)]
        xts = [kb.sb("xt%d" % i, [128, 1024]) for i in range(NB)]
        mots = [kb.sb("mot%d" % i, [128, 4, 128], BF16) for i in range(NB)]
        gsm = kb.sb("gsm", [128, 8])
        sm = kb.sb("sm", [128, 64])
        expgc = kb.sb("expgc", [128, 8])
        egl = kb.sb("egl", [128, 2, 8])
        egf = kb.sb("egf", [64, 2, 8])
        dff = kb.sb("dff", [64, 2, 8])
        kdff = kb.sb("kdff", [64, 2, 8])
        bexp = kb.sb("bexp", [128, 8])
        big = lambda nm: kb.sb(nm, [128, 8, 128])
        dg, tmp, Dm, Am = big("dg"), big("tmp"), big("Dm"), big("Am")
        Xs = [big("Xa"), big("Xb")]
        Ys = [big("Ya"), big("Yb")]
        Rs = [big("Ra"), big("Rb")]
        vb, kbg, wT = big("vb"), big("kbg"), big("wT")
        ATf = kb.sb("ATf", [64, 2, 8, 64])
        kd_f = kb.sb("kd_f", [64, 2, 8, 128])
        u = kb.sb("u", [64, 2, 8, 128])
        vnew = kb.sb("vnew", [64, 8, 128])
        t1 = kb.sb("t1", [64, 8, 128])
        o_f = kb.sb("o_f", [64, 2, 8, 128])
        sqo = kb.sb("sqo", [64, 2, 8, 128])
        ssq = kb.sb("ssq", [64, 16])
        mixT = kb.sb("mixT", [128, 8, 128], BF16)
        xo = kb.sb("xo", [128, 1024])
        for tt_ in range(32):
            b_ = tt_ % NB
            cs = slice(tt_ * 128, (tt_ + 1) * 128)
            kb.dma(SP, qTt[b_][:], T["qT"][:, :, cs].rearrange("h d t -> d h t"))
            kb.dma(ACT, kTt[b_][:], T["kT"][:, :, cs].rearrange("h d t -> d h t"))
            kb.dma(SP, kTMt[b_][:], T["kTM"][cs, :].rearrange("p (h f) -> p h f", h=8))
            kb.dma(ACT, vTMt[b_][:], T["vTM"][cs, :].rearrange("p (h f) -> p h f", h=8))
            kb.dma(SP, kf[b_][:], T["kTM"][cs, :].rearrange("(c p) (h f) -> p c h f", p=64, h=8))
            kb.dma(ACT, zf[b_][:], T["zs"][cs, :].rearrange("(c p) f -> p c f", p=64))
            kb.dma(SP, bgts[b_][:], T["bg"][cs, :])
            kb.dma(ACT, xts[b_][:], T["x"][cs, :])
            kb.dma(SP, mots[b_][:], T["moT"][:, :, cs].rearrange("h p t -> p h t"))
            bg = bgts[b_]
            beta, g = bg[:, 0:8], bg[:, 8:16]
            q_, k_ = qTt[b_], kTt[b_]
            ps = P[0]
            kb.mm(ps[:, 0:8], cn["tri"][:], g)
            kb.cp(DVE, gsm[:], ps[:, 0:8])
            kb.mm(ps[:, 8:16], cn["sellast"][:], gsm[:])
            kb.mm(ps[:, 16:24], cn["sela"][:], gsm[:])
            kb.mm(ps[:, 24:32], cn["selb"][:], gsm[:])
            for c in range(2):
                kb.mm(ps[0:64, 32 + c * 8: 40 + c * 8], cn["tri"][:, c * 64:(c + 1) * 64], g)
                kb.mm(ps[0:64, 48 + c * 8: 56 + c * 8], cn["sellast"][:, c * 64:(c + 1) * 64], gsm[:])
            kb.cp(DVE, sm[:, 8:32], ps[:, 8:32])
            kb.cp(DVE, sm[0:64, 32:64], ps[0:64, 32:64])
            kb.act(expgc[:], gsm[:], AF.Exp)
            kb.act(egl[:].m(lambda a: a.rearrange("p c h -> p (c h)")), sm[:, 16:32], AF.Exp)
            kb.act(egf[:].m(lambda a: a.rearrange("p c h -> p (c h)")), sm[0:64, 32:48], AF.Exp)
            kb.tt(DVE, dff[:].m(lambda a: a.rearrange("p c h -> p (c h)")), sm[0:64, 48:64], sm[0:64, 32:48], ALU.subtract)
            kb.act(kdff[:], dff[:], AF.Exp)
            kb.tt(DVE, bexp[:], beta, expgc[:], ALU.mult)
            kb.tt(DVE, dg[:], bc8(ident[:]), sc8(gsm[:]), ALU.mult)
            for h in range(8):
                kb.mm(P3[1][:, h, :], ones[:], dg[:, h, :])
            kb.tt(DVE, tmp[:], P3[1], bc8(cn["maskneg"][:]), ALU.add)
            for h in range(8):
                kb.act(Dm[:, h, :], tmp[:, h, :], AF.Exp, bias=gsm[:, h:h + 1], scale=-1.0)
            for h in range(8):
                kb.mm(P3[2][:, h, :], k_[:, h, :], k_[:, h, :])
            for h in range(8):
                kb.mm(P3[3][:, h, :], q_[:, h, :], k_[:, h, :])
            X0, Y0, R0 = Xs[0], Ys[0], Rs[0]
            kb.tt(DVE, X0[:], P3[2], Dm[:], ALU.mult)
            kb.tt(POOL, X0[:], X0[:], bc8(cn["strict"][:]), ALU.mult)
            kb.tt(DVE, X0[:], X0[:], sc8(beta), ALU.mult)
            kb.tt(DVE, Am[:], P3[3], Dm[:], ALU.mult)
            for h in range(8):
                kb.tr(P3[0][:, h, :], X0[:, h, :], ident[:])
            kb.cp(ACT, Y0[:], P3[0])
            for c in range(2):
                for h in range(8):
                    kb.tr(P[1][0:64, (c * 8 + h) * 64:(c * 8 + h + 1) * 64],
                          Am[c * 64:(c + 1) * 64, h, c * 64:(c + 1) * 64], ident[c * 64:(c + 1) * 64, c * 64:(c + 1) * 64])
            kb.cp(DVE, ATf[:].m(lambda a: a.rearrange("p c h i -> p (c h i)")), P[1][0:64, :])
            kb.tt(DVE, R0[:], bc8(ident[:]), Y0[:], ALU.subtract)
            Xp, Yp, Rp = X0, Y0, R0
            for l in range(1, 6):
                Xn, Yn, Rn = Xs[l % 2], Ys[l % 2], Rs[l % 2]
                for h in range(8):
                    kb.mm(P3[2][:, h, :], Yp[:, h, :], Xp[:, h, :])
                if l < 5:
                    for h in range(8):
                        kb.mm(P3[3][:, h, :], Xp[:, h, :], Yp[:, h, :])
                kb.cp(ACT, Xn[:], P3[2])
                if l < 5:
                    kb.cp(DVE, Yn[:], P3[3])
                for h in range(8):
                    kb.mm(P3[0][:, h, :], Xn[:, h, :], Rp[:, h, :])
                kb.tt(DVE, Rn[:], Rp[:], P3[0], ALU.add)
                Xp, Yp, Rp = Xn, Yn, Rn
            R = Rp
            kb.tt(POOL, vb[:], vTMt[b_][:], sc8(beta), ALU.mult)
            kb.tt(POOL, kbg[:], kTMt[b_][:], sc8(bexp[:]), ALU.mult)
            kb.tt(POOL, kd_f[:], kf[b_][:], kdff[:].m(lambda a: a.unsqueeze(3).to_broadcast([64, 2, 8, 128])), ALU.mult)
            for c in range(2):
                for h in range(8):
                    kb.mm(P3[2][0:64, h, :], R[:, h, c * 64:(c + 1) * 64], vb[:, h, :])
                kb.cp(ACT, u[:, c, :, :], P3[2][0:64, :, :])
            for h in range(8):
                kb.mm(P3[3][:, h, :], kbg[:, h, :], R[:, h, :])
            kb.cp(DVE, wT[:], P3[3])
            for c in range(2):
                cc = slice(c * 64, (c + 1) * 64)
                for h in range(8):
                    kb.mm(P3[0][0:64, h, :], wT[:, h, cc], S_[:, h, :])
                for h in range(8):
                    kb.mm(P3[1][0:64, h, :], q_[:, h, cc], S_[:, h, :])
                kb.tt(DVE, vnew[:], u[:, c, :, :], P3[0][0:64, :, :], ALU.subtract)
                for h in range(8):
                    kb.mm(P3[2][0:64, h, :], ATf[:, c, h, :], vnew[:, h, :])
                kb.tt(DVE, t1[:], P3[1][0:64, :, :], egf[:, c, :].m(lambda a: a.unsqueeze(2).to_broadcast([64, 8, 128])), ALU.mult)
                kb.tt(DVE, o_f[:, c, :, :], t1[:], P3[2][0:64, :, :], ALU.add)
                for h in range(8):
                    kb.mm(P3[3][:, h, :], kd_f[:, c, h, :], vnew[:, h, :])
                kb.tt(POOL, S_[:], S_[:], sc8(egl[:, c, :]), ALU.mult)
                kb.tt(DVE, S_[:], S_[:], P3[3], ALU.add)
            o16 = o_f[:].m(lambda a: a.rearrange("p c h e -> p (c h) e"))
            kb.tt(POOL, sqo[:], o_f[:], o_f[:], ALU.mult)
            kb.reduce(DVE, ssq[:], sqo[:].m(lambda a: a.rearrange("p c h e -> p (c h) e")), ALU.add)
            kb.act(ssq[:], ssq[:], AF.Sqrt, scale=1.0 / 128, bias=epsc[0:64, 0:1])
            kb.recip(ssq[:], ssq[:])
            kb.tt(DVE, o16, o16, ssq[:].m(lambda a: a.unsqueeze(2).to_broadcast([64, 16, 128])), ALU.mult)
            kb.tt(POOL, o16, o16, nw[0:64, :].m(lambda a: a.unsqueeze(1).to_broadcast([64, 16, 128])), ALU.mult)
            of2 = o_f[:].m(lambda a: a.rearrange("p c h e -> p c (h e)"))
            kb.tt(DVE, of2, of2, zf[b_][:], ALU.mult)
            for c in range(2):
                for fc in range(8):
                    kb.tr(P3[0][:, fc, c * 64:(c + 1) * 64], o_f[:, c, fc, :], ident[0:64, 0:64])
            kb.cp(ACT, mixT[:], P3[0])
            out_proj(kb, wout, mixT, mots[b_], xts[b_], xo, P[1], T["xa"][cs, :])


def phase_ffn(kb, T, i, src, dst, final):
    with kb.phase("ffn%d" % i):
        W = common_work(kb, T)
        wup = load_w_bf16(kb, "wup", T["ffn_w_up"][i], 8, 5632)
        wdn = load_w_bf16(kb, "wdn", T["ffn_w_down"][i], 22, 1024)
        gcol = kb.sb("gcol", [128, 8])
        kb.dma(SP, gcol[:], T["p_gffn%d" % i])
        cw = kb.sb("cw", [128, 3, 22])
        kb.dma(SP, cw[:], T["p_fcw%d" % i])
        cb = kb.sb("cb", [128, 22])
        kb.dma(SP, cb[:], T["p_fcb%d" % i])
        if final:
            fw = kb.sb("fw", [128, 1024])
            kb.dma(SP, fw[:], T["p_fw"])
        halo = kb.sb("halo", [128, 22, 2])
        kb.memset(POOL, halo[:], 0.0)
        PG = [kb.ps("PG%d" % j) for j in range(2)]
        PV = [kb.ps("PV%d" % j) for j in range(2)]
        PO = [kb.ps("PO%d" % j) for j in range(2)]
        ST = 256
        hTs = [kb.sb("hT%d" % j, [128, 8, ST], BF16) for j in range(2)]
        xts = [kb.sb("xt%d" % j, [128, 1024]) for j in range(4)]
        gbs = [kb.sb("gb%d" % j, [128, ST + 2]) for j in range(2)]
        accs = [kb.sb("acc%d" % j, [128, ST]) for j in range(2)]
        sgs = [kb.sb("sg%d" % j, [128, ST]) for j in range(2)]
        actTs = [kb.sb("actT%d" % j, [128, 22, ST], BF16) for j in range(2)]
        xos = [kb.sb("xo%d" % j, [128, 1024]) for j in range(2)]
        NST = S_LEN // ST

        def norm_st(st):
            for a in range(2):
                xt = xts[(st % 2) * 2 + a]
                r0 = st * ST + a * 128
                kb.dma(SP if a == 0 else ACT, xt[:], src[r0:r0 + 128, :])
                rms_to_hT(kb, W, xt[:], gcol, hTs[st % 2], a * 128)

        norm_st(0)
        for st in range(NST):
            hT = hTs[st % 2]
            actT = actTs[st % 2]
            for c in range(22):
                pg, pv = PG[c % 2], PV[c % 2]
                for k in range(8):
                    kb.mm(pg[:, 0:ST], wup[:, k, c * 128:(c + 1) * 128], hT[:, k, :], start=(k == 0), stop=(k == 7))
                for k in range(8):
                    kb.mm(pv[:, 0:ST], wup[:, k, 2816 + c * 128: 2816 + (c + 1) * 128], hT[:, k, :], start=(k == 0), stop=(k == 7))
                gb, acc, sg = gbs[c % 2], accs[c % 2], sgs[c % 2]
                kb.cp(POOL, gb[:, 0:2], halo[:, c, :])
                kb.cp(ACT, gb[:, 2:ST + 2], pg[:, 0:ST])
                kb.cp(POOL, halo[:, c, :], gb[:, ST:ST + 2])
                kb.ts(DVE, acc[:], gb[:, 2:ST + 2], cw[:, 2, c:c + 1], cb[:, c:c + 1], ALU.mult, ALU.add)
                kb.stt(DVE, acc[:], gb[:, 1:ST + 1], cw[:, 1, c:c + 1], acc[:], ALU.mult, ALU.add)
                kb.stt(DVE, acc[:], gb[:, 0:ST], cw[:, 0, c:c + 1], acc[:], ALU.mult, ALU.add)
                kb.act(sg[:], acc[:], AF.Silu)
                kb.tt(DVE, actT[:, c, :], sg[:], pv[:, 0:ST], ALU.mult)
                if c == 10 and st + 1 < NST:
                    norm_st(st + 1)
            for a in range(2):
                xt = xts[(st % 2) * 2 + a]
                xo = xos[a]
                r0 = st * ST + a * 128
                for n in range(2):
                    for c in range(22):
                        kb.mm(PO[n][:], actT[:, c, a * 128:(a + 1) * 128], wdn[:, c, n * 512:(n + 1) * 512], start=(c == 0), stop=(c == 21))
                    kb.tt(DVE, xo[:, n * 512:(n + 1) * 512], xt[:, n * 512:(n + 1) * 512], PO[n][:], ALU.add)
                if final:
                    kb.memset(DVE, W["ss"][:], 0.0)
                    kb.act(W["junk"][:], xo[:], AF.Square, accum=W["ss"][:])
                    kb.act(W["rstd"][:], W["ss"][:], AF.Sqrt, scale=1.0 / DM, bias=W["epsc"][:, 0:1])
                    kb.recip(W["rstd"][:], W["rstd"][:])
                    kb.ts(DVE, xo[:], xo[:], W["rstd"][:, 0:1], None, ALU.mult)
                    kb.tt(POOL, xo[:], xo[:], fw[:], ALU.mult)
                kb.dma(SP, dst[r0:r0 + 128, :], xo[:])


def phase_l1_in(kb, T):
    with kb.phase("l1in"):
        W = common_work(kb, T)
        PA = [kb.ps("PA%d" % i) for i in range(2)]
        PB = kb.ps("PB")
        PS0, PS1, PO = kb.ps("PS0"), kb.ps("PS1"), kb.ps("PO")
        win = load_w_bf16(kb, "win", T["nsa_w_in"], 8, 3096)
        gcol = kb.sb("gcol", [128, 8])
        kb.dma(SP, gcol[:], T["p_gmix1"])
        kmemT, vmem = mem_prep(kb, T, W, 1, PA[0], PA[1])
        rbx = kb.sb("rbx", [33, 8])
        kb.dma(SP, rbx[:], T["p_rbx"])
        ohx = kb.sb("ohx", [33, T_L])
        kb.dma(SP, ohx[:], T["c_ohx"])
        tbl = kb.sb("tbl", [8, T_L], BF16)
        for j in range(0, T_L, 512):
            w_ = min(512, T_L - j)
            kb.mm(PB[0:8, 0:w_], rbx[:], ohx[:, j:j + w_])
            kb.cp(DVE, tbl[:, j:j + w_], PB[0:8, 0:w_])
        kb.dma(SP, T["tblD"], tbl[:])
        hTs = [kb.sb("hT%d" % i, [128, 8, 512], BF16) for i in range(2)]
        xts = [kb.sb("xt%d" % i, [128, 1024]) for i in range(2)]
        stg = [kb.sb("stg%d" % i, [128, 512], BF16) for i in range(2)]
        gst = kb.sb("gst", [24, 512])
        vst = [kb.sb("vst%d" % i, [128, 512], BF16) for i in range(2)]
        qmT = kb.sb("qmT", [128, 4, 512], BF16)
        pTs = [kb.sb("pTs%d" % i, [128, 512], BF16) for i in range(2)]
        rden = kb.sb("rden", [128, 512])
        moTst = kb.sb("moTst", [128, 4, 512], BF16)
        fm = [(h * 128, T["q1T"][h], 128.0 ** -0.5) for h in range(8)]
        for g in range(2):
            fm.append((1024 + g * 128, T["kcT"][g], 1.0))
            fm.append((1280 + g * 128, T["vcT"][g], 1.0))
            fm.append((1536 + g * 128, T["ksT"][g], 1.0))
            fm.append((2048 + g * 128, T["kwT"][g], 1.0))
        def norm_st(st):
            for a in range(4):
                xt = xts[a % 2]
                kb.dma(SP if a % 2 == 0 else ACT, xt[:], T["xb"][st * 512 + a * 128: st * 512 + (a + 1) * 128, :])
                rms_to_hT(kb, W, xt[:], gcol, hTs[st % 2], a * 128)

        norm_st(0)
        for st in range(8):
            hT = hTs[st % 2]
            cs = slice(st * 512, (st + 1) * 512)
            for ci, (c0, dst, scl) in enumerate(fm):
                pa = PA[ci % 2]
                for k in range(8):
                    kb.mm(pa[:], win[:, k, c0:c0 + 128], hT[:, k, :], start=(k == 0), stop=(k == 7))
                sg = stg[ci % 2]
                if ci % 2 == 0:
                    kb.ts(DVE, sg[:], pa[:], scl, None, ALU.mult)
                else:
                    kb.act(sg[:], pa[:], AF.Copy, scale=scl)
                kb.dma(SP if ci % 2 == 0 else ACT, dst[:, cs], sg[:])
                if ci == 8 and st + 1 < 8:
                    norm_st(st + 1)
            for k in range(8):
                kb.mm(PB[0:24, :], win[:, k, 2560:2584], hT[:, k, :], start=(k == 0), stop=(k == 7))
            kb.act(gst[:], PB[0:24, :], AF.Sigmoid)
            kb.dma(SP, T["gT"][:, cs], gst[:])
            for h4 in range(4):
                pa = PA[h4 % 2]
                for k in range(8):
                    kb.mm(pa[:], win[:, k, 2584 + h4 * 128: 2584 + (h4 + 1) * 128], hT[:, k, :], start=(k == 0), stop=(k == 7))
                kb.ts(DVE, qmT[:, h4, :], pa[:], 128.0 ** -0.5, None, ALU.mult)
            mem_attn(kb, W, qmT, kmemT, vmem, moTst, PS0, PS1, PO, PB, pTs, rden)
            kb.dma(SP, T["moT"][:, :, cs].rearrange("h p t -> p h t"), moTst[:])
            for a in range(4):
                ts_ = slice(a * 128, (a + 1) * 128)
                rows = slice(st * 512 + a * 128, st * 512 + (a + 1) * 128)
                pa = PA[a % 2]
                for k in range(8):
                    kb.mm(pa[:, 0:256], hT[:, k, ts_], win[:, k, 1792:2048], start=(k == 0), stop=(k == 7))
                for k in range(8):
                    kb.mm(pa[:, 256:512], hT[:, k, ts_], win[:, k, 2304:2560], start=(k == 0), stop=(k == 7))
                vs_ = vst[a % 2]
                kb.cp(ACT, vs_[:], pa[:])
                kb.dma(SP, T["vs"][rows, :], vs_[:, 0:256])
                kb.dma(ACT, T["vw"][rows, :], vs_[:, 256:512])


def phase_l1_cmp(kb, T):
    with kb.phase("l1cmp"):
        ident = kb.sb("ident", [128, 128])
        kb.dma(SP, ident[:], T["c_ident"])
        identb = kb.sb("identb", [128, 128], BF16)
        kb.cp(DVE, identb[:], ident[:])
        B0, B1, B2 = kb.ps("B0"), kb.ps("B1"), kb.ps("B2")
        PTB = kb.ps("PTB", [128, 1024], BF16)
        kcs = [kb.sb("kcs%d" % g, [128, 256], BF16) for g in range(2)]
        vcs = [kb.sb("vcs%d" % g, [128, 2, 128], BF16) for g in range(2)]
        xc = kb.sb("xc", [128, 4096], BF16)
        w1 = kb.sb("w1", [128, 32, 128], BF16)
        w2 = kb.sb("w2", [128, 128], BF16)
        posf = kb.sb("posf", [128, 32])
        posb = kb.sb("posb", [128, 32], BF16)
        biasv = kb.sb("biasv", [128, 1])
        hs = kb.sb("hs", [128, 256], BF16)
        vccT = kb.sb("vccT", [128, 256], BF16)
        for kind in ("k", "v"):
            kb.dma(POOL, w1[:], T["w1" + kind].rearrange("(j d) h -> d j h", d=128))
            kb.dma(POOL, w2[:], T["w2" + kind])
            kb.dma(SP, posf[:], T["p_pos" + kind])
            kb.cp(DVE, posb[:], posf[:])
            for j in range(32):
                kb.mm(B0[:, 0:1], w1[:, j, :], posb[:, j:j + 1], start=(j == 0), stop=(j == 31))
            kb.cp(DVE, biasv[:], B0[:, 0:1])
            for g in range(2):
                kb.dma(SP, xc[:], T["kcT" if kind == "k" else "vcT"][g])
                for j in range(32):
                    kb.mm(B1[:, 0:255], w1[:, j, :], xc[:, j:j + 16 * 254 + 1:16], start=(j == 0), stop=(j == 31))
                kb.act(hs[:, 0:255], B1[:, 0:255], AF.Silu, bias=biasv[:, 0:1])
                kb.mm(B2[:, 0:255], w2[:], hs[:, 0:255])
                if kind == "k":
                    kb.memset(POOL, kcs[g][:, 255:256], 0.0)
                    kb.cp(ACT, kcs[g][:, 0:255], B2[:, 0:255])
                else:
                    kb.memset(POOL, vccT[:, 255:256], 0.0)
                    kb.cp(ACT, vccT[:, 0:255], B2[:, 0:255])
                    for cc in range(2):
                        kb.tr(PTB[:, cc * 128:(cc + 1) * 128], vccT[:, cc * 128:(cc + 1) * 128], identb[:])
                    kb.cp(DVE, vcs[g][:], PTB[:, 0:256].m(lambda a: a.rearrange("p (c f) -> p c f", c=2)))
        for g in range(2):
            kb.dma(SP, T["kcc"][g], kcs[g][:])
            kb.dma(ACT, T["vcc"][g], vcs[g][:])


def phase_l1_attn(kb, T):
    with kb.phase("l1attn"):
        ident = kb.sb("ident", [128, 128])
        kb.dma(SP, ident[:], T["c_ident"])
        identb = kb.sb("identb", [128, 128], BF16)
        kb.cp(DVE, identb[:], ident[:])
        onesb = kb.sb("onesb", [128, 128], BF16)
        kb.memset(POOL, onesb[:], 1.0)
        B = [kb.ps("B%d" % i) for i in range(7)]
        PTB = kb.ps("PTB", [128, 1024], BF16)
        wout = load_w_bf16(kb, "wout", T["w_out"][1], 12, 1024)
        m4t = kb.sb("m4t", [128, 128], BF16)
        kb.dma(SP, m4t[:], T["c_m4t"])
        E = kb.sb("E", [64, 4096], BF16)
        kb.dma(SP, E[:], T["c_e"])
        selmap = kb.sb("selmap", [128, 2, 65], BF16)
        kb.dma(SP, selmap[:], T["c_selmap"])
        d0 = kb.sb("d0", [128, 64])
        kb.dma(SP, d0[:], T["c_d0"])
        jb100 = kb.sb("jb100", [128, 64])
        kb.dma(SP, jb100[:], T["c_jb100"])
        selrow = kb.sb("selrow", [24, 24, 128])
        kb.dma(SP, selrow[:], T["c_selrow"])
        U = kb.sb("U", [128, 8, U_W], BF16)
        T01 = kb.sb("T01", [128, 8, 256], BF16)
        for ci in range(128):
            o_ = T_NOFF + U_MLO - 16 * ci
            kb.dma(SP if ci % 2 == 0 else ACT, U[ci:ci + 1, :, :], T["tblD"][:, o_:o_ + U_W])
            o2 = T_NOFF - ci
            kb.dma(ACT if ci % 2 == 0 else SP, T01[ci:ci + 1, :, :], T["tblD"][:, o2:o2 + 256])
        ksT = [kb.sb("ksT%d" % g, [128, 4096], BF16) for g in range(2)]
        kwT = [kb.sb("kwT%d" % g, [128, 4096], BF16) for g in range(2)]
        vs = [kb.sb("vs%d" % g, [128, 32, 128], BF16) for g in range(2)]
        vw = [kb.sb("vw%d" % g, [128, 32, 128], BF16) for g in range(2)]
        kcmpT = [kb.sb("kcmpT%d" % g, [128, 256], BF16) for g in range(2)]
        vc = [kb.sb("vc%d" % g, [128, 2, 128], BF16) for g in range(2)]
        for g in range(2):
            kb.dma(SP, ksT[g][:], T["ksT"][g])
            kb.dma(ACT, kwT[g][:], T["kwT"][g])
            kb.dma(SP, vs[g][:], T["vs"][:, g * 128:(g + 1) * 128].rearrange("(t p) f -> p t f", p=128))
            kb.dma(ACT, vw[g][:], T["vw"][:, g * 128:(g + 1) * 128].rearrange("(t p) f -> p t f", p=128))
        for g in range(2):
            kb.dma(SP, kcmpT[g][:], T["kcc"][g])
            kb.dma(ACT, vc[g][:], T["vcc"][g])
        NB = 2
        qts = [kb.sb("qt%d" % i, [128, 8, 128], BF16) for i in range(NB)]
        gts = [kb.sb("gts%d" % i, [24, 128]) for i in range(NB)]
        xts = [kb.sb("xt%d" % i, [128, 1024]) for i in range(NB)]
        mots = [kb.sb("mot%d" % i, [128, 4, 128], BF16) for i in range(NB)]
        pTc = [kb.sb("pTc%d" % i, [128, 4, 128], BF16) for i in range(2)]
        pTs = [kb.sb("pTs%d" % i, [128, 512], BF16) for i in range(2)]
        pTw = [kb.sb("pTw%d" % i, [128, 512], BF16) for i in range(2)]
        dmx = kb.sb("dmx", [128, 4])
        impn = kb.sb("impn", [128, 64])
        t1 = kb.sb("t1", [128, 64])
        t2 = kb.sb("t2", [128, 64])
        sc = kb.sb("sc", [128, 64])
        sc2 = kb.sb("sc2", [128, 64])
        m8 = kb.sb("m8", [128, 16])
        selb = kb.sb("selb", [128, 64])
        selbT = kb.sb("selbT", [64, 128], BF16)
        rd = kb.sb("rd", [128, 128])
        tmpo = kb.sb("tmpo", [128, 128])
        mixf = kb.sb("mixf", [128, 8, 128])
        mixT = kb.sb("mixT", [128, 8, 128], BF16)
        xo = kb.sb("xo", [128, 1024])
        gball = kb.sb("gball", [128, 24, 128])

        def combine(h, b, O, Dn, first):
            kb.ts(DVE, rd[:], Dn, 1e-30, None, ALU.max)
            kb.recip(rd[:], rd[:])
            kb.tt(DVE, rd[:], rd[:], gball[:, h * 3 + b, :], ALU.mult)
            if first:
                kb.tt(DVE, mixf[:, h, :], O, rd[:], ALU.mult)
            else:
                kb.tt(DVE, tmpo[:], O, rd[:], ALU.mult)
                kb.tt(POOL, mixf[:, h, :], mixf[:, h, :], tmpo[:], ALU.add)

        for qt in range(32):
            b_ = qt % NB
            cs = slice(qt * 128, (qt + 1) * 128)
            kb.dma(SP, qts[b_][:], T["q1T"][:, :, cs].rearrange("h d t -> d h t"))
            kb.dma(ACT, gts[b_][:], T["gT"][:, cs])
            kb.dma(SP, xts[b_][:], T["xb"][cs, :])
            kb.dma(ACT, mots[b_][:], T["moT"][:, :, cs].rearrange("h p t -> p h t"))
            qtile = qts[b_]
            for j4 in range(6):
                for i4 in range(4):
                    kb.mm(B[6][:, i4 * 128:(i4 + 1) * 128], selrow[:, j4 * 4 + i4, :], gts[b_][:])
                kb.cp(ACT, gball[:, j4 * 4:(j4 + 1) * 4, :], B[6][:].m(lambda a: a.rearrange("p (i q) -> p i q", i=4)))
            for g in range(2):
                ncc = 2 if qt >= 16 else 1
                for cc in range(ncc):
                    m0 = 128 * qt - 2048 * cc - 31
                    for r in range(4):
                        h = 4 * g + r
                        o_ = B[cc][:, r * 128:(r + 1) * 128]
                        nb = m0 < 2145
                        kb.mm(o_, kcmpT[g][:, cc * 128:(cc + 1) * 128], qtile[:, h, :], start=True, stop=not nb)
                        if nb:
                            kb.mm(o_, identb[:], U[:, h, m0 - U_MLO: m0 - U_MLO + 128], start=False, stop=True)
                    kb.act(pTc[cc][:].m(lambda a: a.rearrange("p r q -> p (r q)")), B[cc][:], AF.Exp)
                for r in range(4):
                    for cc in range(ncc):
                        kb.mm(B[2][:, r * 128:(r + 1) * 128], vc[g][:, cc, :], pTc[cc][:, r, :], start=(cc == 0), stop=(cc == ncc - 1))
                for r in range(4):
                    for cc in range(ncc):
                        kb.mm(B[3][:, r * 128:(r + 1) * 128], onesb[:], pTc[cc][:, r, :], start=(cc == 0), stop=(cc == ncc - 1))
                for r in range(4):
                    for cc in range(ncc):
                        kb.mm(B[4][:, r * 65:(r + 1) * 65], pTc[cc][:, r, :], selmap[:, cc, :], start=(cc == 0), stop=(cc == ncc - 1))
                imp3 = B[4][:, 0:260].m(lambda a: a.rearrange("p (r s) -> p r s", r=4))
                kb.ts(DVE, dmx[:], imp3[:, :, 64], 1e-30, None, ALU.max)
                kb.recip(dmx[:], dmx[:])
                kb.ts(DVE, impn[:], imp3[:, 0, 0:64], dmx[:, 0:1], None, ALU.mult)
                for r in range(1, 4):
                    kb.stt(DVE, impn[:], imp3[:, r, 0:64], dmx[:, r:r + 1], impn[:], ALU.mult, ALU.add)
                kb.ts(DVE, t1[:], d0[:], float(2 * qt - 1), None, ALU.is_ge)
                kb.tt(DVE, t1[:], t1[:], jb100[:], ALU.mult)
                kb.tt(DVE, sc[:], impn[:], t1[:], ALU.max)
                kb.ts(DVE, t2[:], d0[:], float(2 * qt), -1000.0, ALU.is_gt, ALU.mult)
                kb.tt(DVE, sc[:], sc[:], t2[:], ALU.add)
                kb.memset(DVE, sc[:, 0:1], 100.0)
                kb.max8(m8[:, 0:8], sc[:])
                kb.match_replace(sc2[:], m8[:, 0:8], sc[:], -3000.0)
                kb.max8(m8[:, 8:16], sc2[:])
                kb.ts(DVE, t1[:], sc[:], m8[:, 15:16], None, ALU.is_ge)
                kb.ts(DVE, t2[:], sc[:], -500.0, None, ALU.is_gt)
                kb.tt(DVE, t1[:], t1[:], t2[:], ALU.mult)
                kb.ts(DVE, selb[:], t1[:], 1.0, -NEG, ALU.subtract, ALU.mult)
                kb.tr(B[5][0:64, 0:128], selb[:], ident[:])
                kb.cp(DVE, selbT[:], B[5][0:64, 0:128])
                for r in range(4):
                    h = 4 * g + r
                    combine(h, 0, B[2][:, r * 128:(r + 1) * 128], B[3][:, r * 128:(r + 1) * 128], True)
                for r in range(4):
                    h = 4 * g + r
                    nkt = qt + 1
                    OS, DS = B[2][:, 0:128], B[3][:, 0:128]
                    for kg in range((nkt + 3) // 4):
                        kts = list(range(kg * 4, min(nkt, kg * 4 + 4)))
                        sb_ = B[kg % 2]
                        for i_, kt in enumerate(kts):
                            o_ = sb_[:, i_ * 128:(i_ + 1) * 128]
                            near = kt >= qt - 1
                            kb.mm(o_, ksT[g][:, kt * 128:(kt + 1) * 128], qtile[:, h, :], start=True, stop=False)
                            kb.mm(o_, E[:, kt * 128:(kt + 1) * 128], selbT[:], start=False, stop=not near)
                            if kt == qt:
                                kb.mm(o_, identb[:], T01[:, h, 0:128], start=False, stop=True)
                            elif kt == qt - 1:
                                kb.mm(o_, identb[:], T01[:, h, 128:256], start=False, stop=True)
                        nn = len(kts) * 128
                        pt_ = pTs[kg % 2]
                        kb.act(pt_[:, 0:nn], sb_[:, 0:nn], AF.Exp)
                        for i_, kt in enumerate(kts):
                            kb.mm(OS, vs[g][:, kt, :], pt_[:, i_ * 128:(i_ + 1) * 128], start=(kt == 0), stop=(kt == qt))
                        for i_, kt in enumerate(kts):
                            kb.mm(DS, onesb[:], pt_[:, i_ * 128:(i_ + 1) * 128], start=(kt == 0), stop=(kt == qt))
                    combine(h, 1, OS, DS, False)
                    offs = [o for o in range(4, -1, -1) if qt - o >= 0]
                    OW, DW = B[4][:, 0:128], B[5][:, 0:128]
                    slots = []
                    for o in offs:
                        kt = qt - o
                        if o >= 1:
                            bank, col = B[0], (o - 1) * 128
                        else:
                            bank, col = B[1], 0
                        o_ = bank[:, col:col + 128]
                        hasb = o in (0, 1, 4)
                        kb.mm(o_, kwT[g][:, kt * 128:(kt + 1) * 128], qtile[:, h, :], start=True, stop=not hasb)
                        if o == 0:
                            kb.mm(o_, identb[:], T01[:, h, 0:128], start=False, stop=True)
                        elif o == 1:
                            kb.mm(o_, identb[:], T01[:, h, 128:256], start=False, stop=True)
                        elif o == 4:
                            kb.mm(o_, identb[:], m4t[:], start=False, stop=True)
                        slots.append((o, kt))
                    far = [o for o in offs if o >= 1]
                    if far:
                        lo, hi = (min(far) - 1) * 128, max(far) * 128
                        kb.act(pTw[0][:, lo:hi], B[0][:, lo:hi], AF.Exp)
                    kb.act(pTw[1][:, 0:128], B[1][:, 0:128], AF.Exp)
                    pw = lambda o: pTw[0][:, (o - 1) * 128: o * 128] if o >= 1 else pTw[1][:, 0:128]
                    for i_, (o, kt) in enumerate(slots):
                        kb.mm(OW, vw[g][:, kt, :], pw(o), start=(i_ == 0), stop=(i_ == len(slots) - 1))
                    for i_, (o, kt) in enumerate(slots):
                        kb.mm(DW, onesb[:], pw(o), start=(i_ == 0), stop=(i_ == len(slots) - 1))
                    combine(h, 2, OW, DW, False)
            kb.cp(ACT, mixT[:], mixf[:])
            if "mixD" in T:
                kb.dma(SP, T["mixD"][qt], mixf[:])
            for n in range(2):
                o_ = B[n][:]
                for fc in range(8):
                    kb.mm(o_, mixT[:, fc, :], wout[:, fc, n * 512:(n + 1) * 512], start=(fc == 0), stop=False)
                for h4 in range(4):
                    kb.mm(o_, mots[b_][:, h4, :], wout[:, 8 + h4, n * 512:(n + 1) * 512], start=False, stop=(h4 == 3))
                kb.tt(DVE, xo[:, n * 512:(n + 1) * 512], xts[b_][:, n * 512:(n + 1) * 512], o_, ALU.add)
            kb.dma(SP, T["xc"][cs, :], xo[:])


SCRATCH = {
    "qT": ((8, 128, S_LEN), F32), "kT": ((8, 128, S_LEN), F32), "kTM": ((S_LEN, 1024), F32),
    "vTM": ((S_LEN, 1024), F32), "zs": ((S_LEN, 1024), F32), "bg": ((S_LEN, 16), F32),
    "moT": ((4, 128, S_LEN), BF16), "xa": ((S_LEN, DM), F32), "xb": ((S_LEN, DM), F32), "xc": ((S_LEN, DM), F32),
    "q1T": ((8, 128, S_LEN), BF16), "kcT": ((2, 128, S_LEN), BF16), "vcT": ((2, 128, S_LEN), BF16),
    "ksT": ((2, 128, S_LEN), BF16), "kwT": ((2, 128, S_LEN), BF16), "vs": ((S_LEN, 256), BF16),
    "vw": ((S_LEN, 256), BF16), "gT": ((24, S_LEN), F32), "tblD": ((8, T_L), BF16),
    "kcc": ((2, 128, 256), BF16), "vcc": ((2, 128, 2, 128), BF16),
}
NP_DT = {F32: np.float32, BF16: bf}


def build(small_shapes, dbg=(), phases="abcdef"):
    nc = bass.Bass("TRN2", target_bir_lowering=False)
    T = {}
    for name, shp in BIG_INPUTS.items():
        T[name] = nc.dram_tensor(name, list(shp), F32, kind="ExternalInput").ap()
    for name, (shp, dt) in small_shapes.items():
        T[name] = nc.dram_tensor(name, list(shp), dt, kind="ExternalInput").ap()
    for name, (shp, dt) in SCRATCH.items():
        kind = "ExternalOutput" if name in dbg else "Internal"
        T[name] = nc.dram_tensor(name, list(shp), dt, kind=kind).ap()
    T["y"] = nc.dram_tensor("y", [S_LEN, DM], F32, kind="ExternalOutput").ap()
    if "mixD" in dbg:
        T["mixD"] = nc.dram_tensor("mixD", [32, 128, 8, 128], F32, kind="ExternalOutput").ap()
    kb = KB(nc)
    if "a" in phases:
        phase_l0_in(kb, T)
    if "b" in phases:
        phase_l0_gdn(kb, T)
    if "c" in phases:
        phase_ffn(kb, T, 0, T["xa"], T["xb"], False)
    if "d" in phases:
        phase_l1_in(kb, T)
    if "e" in phases:
        phase_l1_cmp(kb, T)
        phase_l1_attn(kb, T)
    if "f" in phases:
        phase_ffn(kb, T, 1, T["xc"], T["y"], True)
    return nc, kb


def prep_inputs(inp):
    shared = {}
    shared["mem_w_kv"] = np.ascontiguousarray(inp["mem_w_kv"], np.float32)
    shared["w_out"] = np.ascontiguousarray(inp["w_out"], np.float32)
    shared["gdn_w_in"] = np.ascontiguousarray(inp["gdn_w_in"][0], np.float32)
    shared["nsa_w_in"] = np.ascontiguousarray(inp["nsa_w_in"][0], np.float32)
    shared["ffn_w_up"] = np.ascontiguousarray(inp["ffn_w_up"], np.float32)
    shared["ffn_w_down"] = np.ascontiguousarray(inp["ffn_w_down"], np.float32)
    shared["w1k"] = np.ascontiguousarray(inp["nsa_cmp_w1_k"][0], np.float32)
    shared["w2k"] = np.ascontiguousarray(inp["nsa_cmp_w2_k"][0], np.float32)
    shared["w1v"] = np.ascontiguousarray(inp["nsa_cmp_w1_v"][0], np.float32)
    shared["w2v"] = np.ascontiguousarray(inp["nsa_cmp_w2_v"][0], np.float32)
    small = {}
    small.update(host_consts())
    small.update(host_layout(inp))
    small_shapes = {k: (v.shape, BF16 if v.dtype == bf else F32) for k, v in small.items()}
    shared.update(small)
    return shared, small_shapes


def kernel(**inp):
    inp = {k: np.asarray(v) for k, v in inp.items()}
    shared, small_shapes = prep_inputs(inp)
    nc, kb = build(small_shapes)
    n = inp["x"].shape[0]
    in_maps = []
    for b in range(n):
        m = dict(shared)
        m["x"] = np.ascontiguousarray(inp["x"][b], np.float32)
        m["mem"] = np.ascontiguousarray(inp["mem"][b], np.float32)
        in_maps.append(m)
    res = run_bass_kernel_spmd(nc, in_maps, core_ids=list(range(n)))
    return np.stack([np.asarray(r["y"], np.float32) for r in res.results], axis=0)
```
